# Optimizing a Trainium2 kernel written in Bass

```python
import jax, jax.numpy as jnp
from jax import lax
import numpy as np

D_MODEL = 1024
BATCH = 16
SEQ = 4096
DEPTH = 1

MEM_LEN = 256
D_MIX = D_MODEL
N_HEADS_GDN = 4
HEAD_DIM_GDN = 128
W_GDN = N_HEADS_GDN * HEAD_DIM_GDN
CONV_K = 4
CHUNK_GDN = 64
N_HEADS_RET = 4
HEAD_DIM_RET = 64
W_RET = N_HEADS_RET * HEAD_DIM_RET
CHUNK_RET = 128
ROPE_BASE = 10000.0
N_HEADS_X = 4
HEAD_DIM_X = 64
W_X = N_HEADS_X * HEAD_DIM_X
EPS = 1e-6
IN_SPLITS = (3 * W_GDN, W_GDN, N_HEADS_GDN, N_HEADS_GDN, W_RET, W_RET, W_RET, W_RET, W_X, W_X)
IN_COLS = sum(IN_SPLITS)

kernel_name = "hybrid_gdn_retention_memxattn_layer"


def rms_norm(x, w):
    xf = x.astype(jnp.float32)
    y = xf * lax.rsqrt(jnp.mean(xf * xf, axis=-1, keepdims=True) + EPS)
    return (y * w.astype(jnp.float32)).astype(x.dtype)


def l2_normalize(x):
    return x * lax.rsqrt(jnp.sum(x * x, axis=-1, keepdims=True) + EPS)


def causal_depthwise_conv(x, w):
    K, C = w.shape
    return lax.conv_general_dilated(
        x, w[:, None, :].astype(x.dtype), window_strides=(1,), padding=((K - 1, 0),),
        dimension_numbers=("NWC", "WIO", "NWC"), feature_group_count=C)


def to_chunks(t, chunk):
    B_, S_, H = t.shape[:3]
    t = t.reshape((B_, S_ // chunk, chunk, H) + t.shape[3:])
    return jnp.moveaxis(t, 3, 1)


def from_chunks(t):
    B_, H, N, C, d = t.shape
    return jnp.moveaxis(t, 1, 3).reshape(B_, N * C, H, d)


def gated_delta_rule(q, k, v, g, beta):
    B_, S_, H, dk = q.shape
    dv = v.shape[-1]
    C = CHUNK_GDN
    q, k, v = to_chunks(q, C), to_chunks(k, C), to_chunks(v, C)
    g, beta = to_chunks(g, C), to_chunks(beta, C)
    G = jnp.cumsum(g, axis=-1)
    causal = jnp.tril(jnp.ones((C, C), dtype=bool))
    strict = jnp.tril(jnp.ones((C, C), dtype=bool), -1)
    gamma = jnp.exp(jnp.where(causal, G[..., :, None] - G[..., None, :], -jnp.inf))
    kk = jnp.einsum("bhncd,bhnmd->bhncm", k, k)
    M = jnp.where(strict, beta[..., :, None] * kk * gamma, 0.0)
    A = jnp.eye(C, dtype=jnp.float32) + M
    rhs = jnp.concatenate([v * beta[..., None], k * (beta * jnp.exp(G))[..., None]], axis=-1)
    sol = lax.linalg.triangular_solve(A, rhs, left_side=True, lower=True, unit_diagonal=True)
    u, w = sol[..., :dv], sol[..., dv:]
    qk = jnp.einsum("bhncd,bhnmd->bhncm", q, k) * gamma
    q_dec = q * jnp.exp(G)[..., None]
    g_last = G[..., -1]
    k_dec = k * jnp.exp(g_last[..., None] - G)[..., None]

    def step(S, inp):
        u_n, w_n, qk_n, qd_n, kd_n, gl_n = inp
        v_new = u_n - jnp.einsum("bhck,bhkv->bhcv", w_n, S)
        o = jnp.einsum("bhck,bhkv->bhcv", qd_n, S) + jnp.einsum("bhcm,bhmv->bhcv", qk_n, v_new)
        S = S * jnp.exp(gl_n)[..., None, None] + jnp.einsum("bhck,bhcv->bhkv", kd_n, v_new)
        return S, o

    xs = tuple(jnp.moveaxis(t, 2, 0) for t in (u, w, qk, q_dec, k_dec, g_last))
    S0 = jnp.zeros((B_, H, dk, dv), jnp.float32)
    _, o = lax.scan(step, S0, xs)
    return from_chunks(jnp.moveaxis(o, 0, 2))


def rotate_every_two(x):
    x1 = x[..., 0::2]
    x2 = x[..., 1::2]
    return jnp.stack((-x2, x1), axis=-1).reshape(x.shape)


def retention_rotary(x):
    S_, d = x.shape[1], x.shape[-1]
    angle = 1.0 / (ROPE_BASE ** jnp.linspace(0.0, 1.0, d // 2, dtype=jnp.float32))
    angle = jnp.repeat(angle, 2)
    theta = jnp.arange(S_, dtype=jnp.float32)[:, None] * angle[None, :]
    return x * jnp.cos(theta)[:, None, :] + rotate_every_two(x) * jnp.sin(theta)[:, None, :]


def chunkwise_retention(q, k, v):
    B_, S_, H, d = q.shape
    dv = v.shape[-1]
    C = CHUNK_RET
    log_gamma = jnp.log(1.0 - 2.0 ** (-5.0 - jnp.arange(H, dtype=jnp.float32)))
    q, k, v = to_chunks(q, C), to_chunks(k, C), to_chunks(v, C)
    idx = jnp.arange(C, dtype=jnp.float32)
    diff = idx[:, None] - idx[None, :]
    Dmat = jnp.where(diff >= 0, jnp.exp(log_gamma[:, None, None] * jnp.maximum(diff, 0.0)), 0.0)
    scores = jnp.einsum("bhncd,bhnmd->bhncm", q, k) * Dmat[:, None]
    o_inner = jnp.einsum("bhncm,bhnmv->bhncv", scores, v)
    xi = jnp.exp(log_gamma[:, None] * (idx + 1.0))
    zeta = jnp.exp(log_gamma[:, None] * (C - 1.0 - idx))
    kv = jnp.einsum("bhncd,bhncv->bhndv", k * zeta[:, None, :, None], v)
    chunk_decay = jnp.exp(log_gamma * C)[:, None, None]

    def step(R, kv_n):
        return R * chunk_decay + kv_n, R

    R0 = jnp.zeros((B_, H, d, dv), jnp.float32)
    _, R_prev = lax.scan(step, R0, jnp.moveaxis(kv, 2, 0))
    R_prev = jnp.moveaxis(R_prev, 0, 2)
    o_cross = jnp.einsum("bhncd,bhndv->bhncv", q, R_prev) * xi[:, None, :, None]
    return from_chunks(o_inner + o_cross)


def hybrid_layer(x, mem, norm_w, w_in, gdn_conv_w, gdn_A_log, gdn_dt_bias, gdn_norm_w,
                 ret_gn_w, mem_norm_w, w_mem_kv, w_out):
    B_, S_, _ = x.shape
    f32 = jnp.float32
    h = rms_norm(x, norm_w)
    proj = h @ w_in.astype(x.dtype)
    split_idx = [int(i) for i in np.cumsum(IN_SPLITS)[:-1]]
    (gdn_qkv, gdn_z, gdn_b, gdn_a, ret_q, ret_k, ret_v, ret_z, x_q, x_z) = jnp.split(proj, split_idx, axis=-1)

    qkv = jax.nn.silu(causal_depthwise_conv(gdn_qkv, gdn_conv_w)).astype(f32)
    q_a, k_a, v_a = [t.reshape(B_, S_, N_HEADS_GDN, HEAD_DIM_GDN) for t in jnp.split(qkv, 3, axis=-1)]
    q_a = l2_normalize(q_a) * (HEAD_DIM_GDN ** -0.5)
    k_a = l2_normalize(k_a)
    g_a = -jnp.exp(gdn_A_log.astype(f32)) * jax.nn.softplus(gdn_a.astype(f32) + gdn_dt_bias.astype(f32))
    beta_a = jax.nn.sigmoid(gdn_b.astype(f32))
    o_a = gated_delta_rule(q_a, k_a, v_a, g_a, beta_a)
    o_a = rms_norm(o_a, gdn_norm_w).reshape(B_, S_, W_GDN)
    y_a = (o_a * jax.nn.silu(gdn_z.astype(f32))).astype(x.dtype)

    q_b = retention_rotary(ret_q.astype(f32).reshape(B_, S_, N_HEADS_RET, HEAD_DIM_RET))
    k_b = retention_rotary(ret_k.astype(f32).reshape(B_, S_, N_HEADS_RET, HEAD_DIM_RET)) * (HEAD_DIM_RET ** -0.5)
    v_b = ret_v.astype(f32).reshape(B_, S_, N_HEADS_RET, HEAD_DIM_RET)
    o_b = chunkwise_retention(q_b, k_b, v_b)
    mu = jnp.mean(o_b, axis=-1, keepdims=True)
    var = jnp.mean(jnp.square(o_b - mu), axis=-1, keepdims=True)
    o_b = ((o_b - mu) * lax.rsqrt(var + EPS)).reshape(B_, S_, W_RET) * ret_gn_w.astype(f32)
    y_b = (o_b * jax.nn.silu(ret_z.astype(f32))).astype(x.dtype)

    mem_h = rms_norm(mem, mem_norm_w)
    kv_m = mem_h @ w_mem_kv.astype(mem.dtype)
    k_m, v_m = jnp.split(kv_m, 2, axis=-1)
    M_ = mem.shape[1]
    k_m = k_m.reshape(B_, M_, N_HEADS_X, HEAD_DIM_X)
    v_m = v_m.reshape(B_, M_, N_HEADS_X, HEAD_DIM_X)
    q_c = x_q.reshape(B_, S_, N_HEADS_X, HEAD_DIM_X)
    logits = jnp.einsum("bshd,bmhd->bhsm", q_c, k_m).astype(f32) * (HEAD_DIM_X ** -0.5)
    p = jax.nn.softmax(logits, axis=-1)
    o_c = jnp.einsum("bhsm,bmhd->bshd", p, v_m.astype(f32)).reshape(B_, S_, W_X)
    y_c = (o_c * jax.nn.silu(x_z.astype(f32))).astype(x.dtype)

    y = jnp.concatenate([y_a, y_b, y_c], axis=-1) @ w_out.astype(x.dtype)
    return x + y


def setup_inputs(seed: int = 0) -> dict:
    key = jax.random.key(seed)
    ks = jax.random.split(key, 16)
    f32 = jnp.float32
    x = jax.random.normal(ks[0], (BATCH, SEQ, D_MODEL), f32)
    mem = jax.random.normal(ks[1], (BATCH, MEM_LEN, D_MODEL), f32)
    norm_w = 1.0 + 0.02 * jax.random.normal(ks[2], (DEPTH, D_MODEL), f32)
    w_in = jax.random.normal(ks[3], (DEPTH, D_MODEL, IN_COLS), f32) * (D_MODEL ** -0.5)
    gdn_conv_w = jax.random.normal(ks[4], (DEPTH, CONV_K, 3 * W_GDN), f32) * (CONV_K ** -0.5)
    gdn_A_log = jnp.log(jax.random.uniform(ks[5], (DEPTH, N_HEADS_GDN), f32, 1.0, 16.0))
    dt = jnp.exp(jax.random.uniform(ks[6], (DEPTH, N_HEADS_GDN), f32, float(np.log(1e-3)), float(np.log(1e-1))))
    gdn_dt_bias = dt + jnp.log(-jnp.expm1(-dt))
    gdn_norm_w = 1.0 + 0.02 * jax.random.normal(ks[7], (DEPTH, HEAD_DIM_GDN), f32)
    ret_gn_w = 1.0 + 0.02 * jax.random.normal(ks[8], (DEPTH, W_RET), f32)
    mem_norm_w = 1.0 + 0.02 * jax.random.normal(ks[9], (DEPTH, D_MODEL), f32)
    w_mem_kv = jax.random.normal(ks[10], (DEPTH, D_MODEL, 2 * W_X), f32) * (D_MODEL ** -0.5)
    w_out = jax.random.normal(ks[11], (DEPTH, D_MIX, D_MODEL), f32) * (D_MIX ** -0.5)
    final_norm_w = 1.0 + 0.02 * jax.random.normal(ks[12], (D_MODEL,), f32)
    return {"x": x, "mem": mem, "norm_w": norm_w, "w_in": w_in, "gdn_conv_w": gdn_conv_w,
            "gdn_A_log": gdn_A_log, "gdn_dt_bias": gdn_dt_bias, "gdn_norm_w": gdn_norm_w,
            "ret_gn_w": ret_gn_w, "mem_norm_w": mem_norm_w, "w_mem_kv": w_mem_kv,
            "w_out": w_out, "final_norm_w": final_norm_w}


def reference(x, mem, norm_w, w_in, gdn_conv_w, gdn_A_log, gdn_dt_bias, gdn_norm_w,
              ret_gn_w, mem_norm_w, w_mem_kv, w_out, final_norm_w):
    h = x
    for l in range(DEPTH):
        h = hybrid_layer(h, mem, norm_w[l], w_in[l], gdn_conv_w[l], gdn_A_log[l], gdn_dt_bias[l],
                         gdn_norm_w[l], ret_gn_w[l], mem_norm_w[l], w_mem_kv[l], w_out[l])
    return rms_norm(h, final_norm_w)
```

```python
import numpy as np
from contextlib import ExitStack
import concourse.bass as bass
import concourse.mybir as mybir
from concourse.bass_utils import run_bass_kernel_spmd

F32 = mybir.dt.float32
BF16 = mybir.dt.bfloat16
AF = mybir.ActivationFunctionType
ALU = mybir.AluOpType

D = 1024
NCORES = 8
EPS = 1e-6
IN_COLS = 3592
MEM = 256


class Buf:
    def __init__(self, prog, t, name, dma_sem=None):
        self.prog, self.t, self.name = prog, t, name
        self.last_w = None
        self.readers = []
        self.dma_sem = dma_sem
        self.dma_cnt = 0

    def __getitem__(self, k):
        return TAP(self, self.t[k])

    def ap(self):
        return TAP(self, self.t[:])


class TAP:
    def __init__(self, buf, ap):
        self.buf, self.ap = buf, ap

    def __getitem__(self, k):
        return TAP(self.buf, self.ap[k])

    def rearrange(self, s, **kw):
        return TAP(self.buf, self.ap.rearrange(s, **kw))

    def unsqueeze(self, a):
        return TAP(self.buf, self.ap.unsqueeze(a))

    def to_broadcast(self, shp):
        return TAP(self.buf, self.ap.to_broadcast(shp))

    def bitcast(self, dt):
        return TAP(self.buf, self.ap.bitcast(dt))


def _ap(x):
    return x.ap if isinstance(x, TAP) else x


def _bufs(xs):
    out = []
    for x in xs:
        if isinstance(x, TAP):
            if x.buf not in out:
                out.append(x.buf)
        elif isinstance(x, Buf):
            if x not in out:
                out.append(x)
    return out


class Prog:
    ENGS = ("pe", "act", "dve", "pool", "sp")

    def __init__(self, nc, es):
        self.nc, self.es = nc, es
        self.sems = {e: es.enter_context(nc.semaphore("s_" + e)) for e in self.ENGS}
        self.cnt = {e: 0 for e in self.ENGS}
        self.seen = {e: {} for e in self.ENGS}
        self.stream = {e: [] for e in self.ENGS}
        self.nbuf = 0

    def sb(self, name, shape, dt, dma=False):
        t = self.es.enter_context(self.nc.sbuf_tensor("t_" + name, list(shape), dt))
        ds = self.es.enter_context(self.nc.semaphore("d_" + name)) if dma else None
        return Buf(self, t, name, ds)

    def ps(self, name, shape, dt):
        t = self.es.enter_context(self.nc.psum_tensor("p_" + name, list(shape), dt))
        return Buf(self, t, name)

    def op(self, eng, fn, r=(), w=()):
        reads, writes = _bufs(r), _bufs(w)
        deps = []
        for b in reads:
            if b.last_w is not None:
                deps.append(b.last_w)
        for b in writes:
            if b.last_w is not None:
                deps.append(b.last_w)
            for rr in b.readers:
                deps.append(rr)
        need = {}
        for (sk, val, e2) in deps:
            if e2 == eng and eng == "pe":
                continue
            if self.seen[eng].get(id(sk), 0) < val:
                if need.get(id(sk), (None, 0))[1] < val:
                    need[id(sk)] = (sk, val)
        waits = list(need.values())
        for sk, val in waits:
            self.seen[eng][id(sk)] = val
        self.cnt[eng] += 1
        ev = (self.sems[eng], self.cnt[eng], eng)
        self.stream[eng].append((waits, fn, self.sems[eng], 1))
        for b in writes:
            b.last_w = ev
            b.readers = []
        for b in reads:
            if b not in writes:
                b.readers.append(ev)
        return ev

    def dma(self, out, in_, eng="sp"):
        ob = out.buf if isinstance(out, TAP) else None
        ib = in_.buf if isinstance(in_, TAP) else None
        tb = ob if ob is not None else ib
        assert tb is not None and tb.dma_sem is not None, "dma buffer needs dma=True"
        reads = [ib] if ib is not None else []
        writes = [ob] if ob is not None else []
        deps = []
        for b in reads:
            if b.last_w is not None:
                deps.append(b.last_w)
        for b in writes:
            if b.last_w is not None:
                deps.append(b.last_w)
            deps.extend(b.readers)
        need = {}
        for (sk, val, e2) in deps:
            if self.seen[eng].get(id(sk), 0) < val:
                if need.get(id(sk), (None, 0))[1] < val:
                    need[id(sk)] = (sk, val)
        waits = list(need.values())
        for sk, val in waits:
            self.seen[eng][id(sk)] = val
        tb.dma_cnt += 16
        ev = (tb.dma_sem, tb.dma_cnt, "dma")
        oa, ia = _ap(out), _ap(in_)
        self.stream[eng].append((waits, lambda e: e.dma_start(out=oa, in_=ia), tb.dma_sem, 16))
        for b in writes:
            b.last_w = ev
            b.readers = []
        for b in reads:
            b.readers.append(ev)
        return ev

    def finish_wait(self, eng, evs):
        self.stream[eng].append(([(sk, val) for (sk, val, _) in evs], None, None, 0))

    def emit(self):
        nc = self.nc
        block = self.es.enter_context(nc.Block())
        semeng = {id(self.sems[e]): e for e in self.ENGS}
        needed = {e: set() for e in self.ENGS}
        for e in self.ENGS:
            for waits, fn, sem, inc in self.stream[e]:
                for sk, val in waits:
                    if id(sk) in semeng:
                        needed[semeng[id(sk)]].add(val)
        incmap = {}
        for e in self.ENGS:
            m, c, seq = {}, 0, 0
            for waits, fn, sem, inc in self.stream[e]:
                if fn is None or inc != 1:
                    continue
                seq += 1
                if seq in needed[e]:
                    c += 1
                m[seq] = c
            incmap[e] = m

        def replay(name):
            def body(e):
                seq = 0
                for waits, fn, sem, inc in self.stream[name]:
                    for sk, val in waits:
                        if id(sk) in semeng:
                            e.wait_ge(sk, incmap[semeng[id(sk)]][val])
                        else:
                            e.wait_ge(sk, val)
                    if fn is not None:
                        ins = fn(e)
                        if inc == 1:
                            seq += 1
                            if seq in needed[name]:
                                ins.then_inc(sem, 1)
                        else:
                            ins.then_inc(sem, inc)
            return body

        block.tensor(replay("pe"))
        block.scalar(replay("act"))
        block.vector(replay("dve"))
        block.gpsimd(replay("pool"))
        block.sync(replay("sp"))


def _consts(S):
    NT = S // 128
    p = np.arange(128)
    c = {}
    c["ident"] = np.eye(128, dtype=np.float32)
    c["ule"] = (p[:, None] <= p[None, :]).astype(np.float32)
    c["mgt"] = (p[:, None] > p[None, :]).astype(np.float32)
    c["ones"] = np.ones((128, 128), np.float32)
    c["mus"] = (p[None, :] > p[:, None]).astype(np.float32)
    c["mui"] = (p[None, :] >= p[:, None]).astype(np.float32)
    H = 4
    lg = np.log(np.float32(1.0) - np.float32(2.0) ** (-5.0 - np.arange(H, dtype=np.float32))).astype(np.float32)
    diff = (p[None, :] - p[:, None]).astype(np.float32)
    dm = np.zeros((128, H, 128), np.float32)
    for h in range(H):
        dm[:, h, :] = np.where(diff >= 0, np.exp(lg[h] * np.maximum(diff, 0.0)), 0.0) * 0.125
    c["dmat"] = dm.reshape(128, 512)
    zeta = np.exp(lg[None, :] * (127.0 - p[:, None].astype(np.float32))).astype(np.float32) * 0.125
    c["zeta"] = zeta
    xi = np.zeros((128, 2, 128), np.float32)
    dec = np.zeros((128, 2), np.float32)
    for pair in range(2):
        for h2 in range(2):
            h = pair * 2 + h2
            xi[h2 * 64:(h2 + 1) * 64, pair, :] = np.exp(lg[h] * (p.astype(np.float32) + 1.0))[None, :]
            dec[h2 * 64:(h2 + 1) * 64, pair] = np.exp(lg[h] * 128.0)
    xi0 = xi.copy(); xi0[64:128] = 0.0
    xi1 = xi.copy(); xi1[0:64] = 0.0
    c["xi0"] = xi0.reshape(128, 256)
    c["xi1"] = xi1.reshape(128, 256)
    hm = np.zeros((128, 2), np.float32); hm[0:64, 0] = 1.0; hm[64:128, 1] = 1.0
    c["hm"] = hm
    c["dec"] = dec
    c["mhalf"] = np.full((128, 8), -0.5, np.float32)
    names = ["ident", "ule", "mgt", "ones", "mus", "mui", "dmat", "zeta", "xi0", "xi1", "hm", "dec", "mhalf"]
    offs, o = {}, 0
    for n in names:
        offs[n] = (o, c[n].shape[1])
        o += c[n].shape[1]
    cat = np.concatenate([c[n] for n in names], axis=1).astype(np.float32)
    ang = (1.0 / (np.float32(10000.0) ** np.linspace(0.0, 1.0, 32, dtype=np.float32))).astype(np.float32)
    ang = np.repeat(ang, 2)
    pos = np.arange(S, dtype=np.float32)
    theta = (pos[:, None] * ang[None, :]).astype(np.float32)
    cos = np.cos(theta.astype(np.float64)).astype(np.float32)
    sin = np.sin(theta.astype(np.float64)).astype(np.float32)
    rope = np.concatenate([cos, -sin[:, 0::2], sin[:, 1::2]], axis=1).reshape(NT, 128, 128).astype(np.float32)
    return cat, offs, np.ascontiguousarray(rope)


def build(NSEQ, S):
    NT = S // 128
    cat, offs, rope_np = _consts(S)
    NC_ = cat.shape[1]
    nc = bass.Bass("TRN2", target_bir_lowering=False)
    dram = lambda n, s, k: nc.dram_tensor(n, list(s), F32, kind=k).ap()
    x_d = dram("x", [NSEQ * S, D], "ExternalInput")
    mem_d = dram("mem", [NSEQ * MEM, D], "ExternalInput")
    win_d = dram("w_in", [D, IN_COLS], "ExternalInput")
    wout_d = dram("w_out", [D, D], "ExternalInput")
    wmem_d = dram("w_mem_kv", [D, 512], "ExternalInput")
    normw_d = dram("norm_w", [D], "ExternalInput")
    mnormw_d = dram("mem_norm_w", [D], "ExternalInput")
    convw_d = dram("gdn_conv_w", [4, 1536], "ExternalInput")
    alog_d = dram("gdn_A_log", [4], "ExternalInput")
    dtb_d = dram("gdn_dt_bias", [4], "ExternalInput")
    gnw_d = dram("gdn_norm_w", [128], "ExternalInput")
    rgw_d = dram("ret_gn_w", [256], "ExternalInput")
    fnw_d = dram("final_norm_w", [D], "ExternalInput")
    cst_d = dram("cst", [128, NC_], "ExternalInput")
    rope_d = dram("rope", [NT, 128, 128], "ExternalInput")
    out_d = dram("out", [NSEQ * S, D], "ExternalOutput")

    es = ExitStack()
    P = Prog(nc, es)
    sb, ps = P.sb, P.ps

    cst = sb("cst", [128, NC_], F32, dma=True)
    def C(name):
        o, n = offs[name]
        return cst[:, o:o + n]
    wtok = sb("wtok", [128, 8, 1800], BF16)
    wf = sb("wf", [128, 8, 1792], BF16)
    wo = sb("wo", [128, 8, 1024], BF16)
    wm = sb("wm", [128, 8, 512], BF16)
    cdiag = sb("cdiag", [128, 48, 128], BF16)
    identb = sb("identb", [128, 128], BF16)
    fw = sb("fw", [128, 1024], F32, dma=True)
    sm = sb("sm", [128, 64], F32, dma=True)
    sm2 = sb("sm2", [128, 16], F32, dma=True)
    rs = sb("rs", [128, 8], F32)
    negA = sb("negA", [128, 4], F32)
    stg = [sb("stg%d" % i, [128, 1024], F32, dma=True) for i in range(4)]
    rope = [sb("rope%d" % i, [128, 128], F32, dma=True) for i in range(4)]

    junkF = sb("junkF", [128, 128], BF16)
    junkB = sb("junkB", [128, 128], BF16)
    hb = sb("hb", [128, 1024], BF16)
    hT = sb("hT", [128, 8, 128], BF16)
    pcT = [sb("pcT%d" % i, [128, 12, 131], BF16) for i in range(2)]
    th = sb("th", [128, 512], F32)
    q2k2 = sb("q2k2", [128, 1024], BF16)
    v2_ = [sb("v2_%d" % i, [128, 512], BF16) for i in range(2)]
    qh = sb("qh", [128, 512], BF16)
    kh = sb("kh", [128, 512], BF16)
    qt = sb("qt", [128, 512], BF16)
    kd_ = [sb("kd_%d" % i, [128, 512], BF16) for i in range(2)]
    ke_ = [sb("ke_%d" % i, [128, 512], BF16) for i in range(2)]
    qkT_ = [sb("qkT_%d" % i, [128, 12, 128], BF16) for i in range(2)]
    Rg = sb("Rg", [128, 4, 128], F32)
    gts = sb("gts", [128, 512], F32)
    gti = sb("gti", [128, 512], F32)
    AP_ = [sb("AP%d" % i, [128, 4, 2, 128], F32) for i in range(2)]
    Bm = [sb("Bm%d" % i, [128, 4, 128], F32) for i in range(2)]
    TT = sb("TT", [128, 4, 128], BF16)
    PT = sb("PT", [128, 4, 128], BF16)
    bu = sb("bu", [128, 512], BF16)
    wT = sb("wT", [128, 4, 128], BF16)
    vn = sb("vn", [128, 512], BF16)
    S32 = sb("S32", [128, 4, 128], F32)
    Sbf = sb("Sbf", [128, 4, 128], BF16)
    qkb_ = [sb("qkb_%d" % i, [128, 512], BF16) for i in range(2)]
    r1 = sb("r1", [128, 512], BF16)
    r2 = sb("r2", [128, 512], BF16)
    qkr = sb("qkr", [128, 512], BF16)
    rT = sb("rT", [128, 10, 128], BF16)
    PbT = sb("PbT", [128, 4, 128], BF16)
    vb_ = [sb("vb_%d" % i, [128, 256], BF16) for i in range(2)]
    vbz_ = [sb("vbz_%d" % i, [128, 256], BF16) for i in range(2)]
    R32 = sb("R32", [128, 2, 64], F32)
    Rbf = sb("Rbf", [128, 2, 64], BF16)
    ynb = sb("ynb", [128, 256], F32)
    qcT_ = [sb("qcT_%d" % i, [128, 4, 128], BF16) for i in range(2)]
    pT = sb("pT", [128, 8, 128], BF16)
    kmT = sb("kmT", [128, 2, 256], BF16)
    vme = sb("vme", [128, 2, 4, 65], BF16)
    gz_ = [sb("gz_%d" % i, [128, 1024], BF16) for i in range(2)]
    y = sb("y", [128, 1024], BF16)
    yT = sb("yT", [128, 8, 128], BF16)
    mhT = AP_[0].ap().rearrange("p h a c -> p (h a c)").bitcast(BF16).rearrange("p (k m) -> p k m", k=8)
    banks = [ps("bank%d" % i, [128, 512], F32) for i in range(8)]
    bkc = [0]

    free_banks = list(banks)

    def bank():
        assert free_banks, "out of PSUM banks"
        return free_banks.pop(0)

    def rel(*bs):
        for b in bs:
            assert b not in free_banks
            free_banks.append(b)

    STN = dict(ssq=1, rstd=1, ss8=8, r8=8, xa=4, e=4, sp=4, g=4, tb=4, hb=4, nb=4, G=4, eG=4, eGl=4, dG=4,
               eGd=4, sq=4, skd=4, ske=4, so=4, ro=4, so2=4, ro2=4, rden=4, ssq2=1, rstd2=1, tmp=4, tmp2=4)
    stb = [{k: sb("st%d_%s" % (i, k), [128, n], F32) for k, n in STN.items()} for i in range(2)]
    par = [0]

    def stc(name):
        return stb[0][name].ap()

    mhalf = C("mhalf")

    def act(out, in_, func, scale=1.0, bias=None, accum=None, extra_r=()):
        kw = {}
        if bias is not None:
            kw["bias"] = _ap(bias)
        if accum is not None:
            kw["accum_out"] = _ap(accum)
        sc = _ap(scale)
        P.op("act", lambda e: e.activation(out=_ap(out), in_=_ap(in_), func=func, scale=sc, **kw),
             r=[in_, scale, bias] + list(extra_r), w=[out, accum])

    def tt(eng, out, in0, in1, op):
        P.op(eng, lambda e: e.tensor_tensor(out=_ap(out), in0=_ap(in0), in1=_ap(in1), op=op), r=[in0, in1], w=[out])

    def ts(out, in0, s1, op0, s2=None, op1=None, eng="dve"):
        if op1 is None:
            P.op(eng, lambda e: e.tensor_scalar(out=_ap(out), in0=_ap(in0), scalar1=_ap(s1), scalar2=None, op0=op0),
                 r=[in0, s1], w=[out])
        else:
            P.op(eng, lambda e: e.tensor_scalar(out=_ap(out), in0=_ap(in0), scalar1=_ap(s1), scalar2=_ap(s2),
                                               op0=op0, op1=op1), r=[in0, s1, s2], w=[out])

    def stt(out, in0, scalar, in1, op0, op1, eng="dve"):
        P.op(eng, lambda e: e.scalar_tensor_tensor(out=_ap(out), in0=_ap(in0), scalar=_ap(scalar), in1=_ap(in1),
                                                   op0=op0, op1=op1), r=[in0, scalar, in1], w=[out])

    def cp(eng, out, in_):
        if eng == "act":
            P.op("act", lambda e: e.copy(out=_ap(out), in_=_ap(in_)), r=[in_], w=[out])
        else:
            P.op(eng, lambda e: e.tensor_copy(out=_ap(out), in_=_ap(in_)), r=[in_], w=[out])

    def memset(eng, out, val):
        P.op(eng, lambda e: e.memset(_ap(out), val), r=[], w=[out])

    def pe(items, r, w):
        def fn(e):
            ins = None
            for it in items:
                if it[0] == "mm":
                    ins = e.matmul(_ap(it[1]), lhsT=_ap(it[2]), rhs=_ap(it[3]), start=it[4], stop=it[5])
                else:
                    ins = e.transpose(out=_ap(it[1]), in_=_ap(it[2]), identity=_ap(it[3]))
            return ins
        P.op("pe", fn, r=r, w=w)

    def rsqrt(out, in_):
        n = in_.ap.shape[-1]
        tt("pool", out, in_, mhalf[:, 0:n], ALU.pow)

    P.dma(cst.ap(), cst_d)
    P.dma(fw.ap(), fnw_d.partition_broadcast(128))
    with nc.allow_non_contiguous_dma(reason="tiny param loads"):
        P.dma(sm[:, 0:8], normw_d.rearrange("(k p) -> p k", p=128))
        P.dma(sm[:, 8:16], mnormw_d.rearrange("(k p) -> p k", p=128))
        for t in range(4):
            P.dma(sm[:, 16 + t * 12:16 + (t + 1) * 12], convw_d[t].rearrange("(c p) -> p c", p=128))
        P.dma(sm2[:, 0:1], gnw_d.rearrange("(p o) -> p o", o=1))
        P.dma(sm2[:, 1:3], rgw_d.rearrange("(k p) -> p k", p=128))
        P.dma(sm2[:, 4:8], dtb_d.partition_broadcast(128))
        P.dma(sm2[:, 8:12], alog_d.partition_broadcast(128))
    cp("dve", identb.ap(), C("ident"))
    act(negA.ap(), sm2[:, 8:12], AF.Exp)
    ts(negA.ap(), negA.ap(), -1.0, ALU.mult)
    for k in range(4):
        ts(rs[:, k:k + 1], sm2[:, 0:1], 0.5, ALU.mult)
    ts(rs[:, 4:6], sm2[:, 1:3], 0.5, ALU.mult)
    memset("dve", rs[:, 6:8], 0.5)
    for cch in range(12):
        for t in range(4):
            ts(cdiag[:, cch * 4 + t, :], identb.ap(), sm[:, 16 + t * 12 + cch:17 + t * 12 + cch], ALU.mult)
    free_stg = list(stg)

    def stage(src_ap):
        assert free_stg, "out of staging slots"
        s = free_stg.pop(0)
        P.dma(s[:, 0:src_ap.shape[1]], src_ap)
        return s

    def rel_stg(s):
        assert s not in free_stg
        free_stg.append(s)

    for k in range(8):
        rows = slice(k * 128, (k + 1) * 128)
        nwk = sm[:, k:k + 1]
        s = stage(win_d[rows, 0:1024]);    ts(wf[:, k, 0:1024], s[:, 0:1024], nwk, ALU.mult); rel_stg(s)
        s = stage(win_d[rows, 1024:2048])
        ts(wf[:, k, 1024:1536], s[:, 0:512], nwk, ALU.mult)
        ts(wtok[:, k, 0:512], s[:, 512:1024], nwk, ALU.mult); rel_stg(s)
        s = stage(win_d[rows, 2048:3072])
        ts(wtok[:, k, 1792:1800], s[:, 0:8], nwk, ALU.mult)
        ts(wtok[:, k, 512:1528], s[:, 8:1024], nwk, ALU.mult); rel_stg(s)
        s = stage(win_d[rows, 3072:3592])
        ts(wtok[:, k, 1528:1536], s[:, 0:8], nwk, ALU.mult)
        ts(wf[:, k, 1536:1792], s[:, 8:264], nwk, ALU.mult)
        ts(wtok[:, k, 1536:1792], s[:, 264:520], nwk, ALU.mult); rel_stg(s)
        s = stage(wout_d[rows, :]);        ts(wo[:, k, :], s[:, 0:1024], rs[:, k:k + 1], ALU.mult); rel_stg(s)
        s = stage(wmem_d[rows, :]);        ts(wm[:, k, :], s[:, 0:512], sm[:, 8 + k:9 + k], ALU.mult); rel_stg(s)

    tile_idx = [0]

    def norm_transpose(src, dstT, dst_cols, stc):
        act(hb.ap(), src, AF.Square, accum=stc("ssq"))
        ts(stc("tmp")[:, 0:1], stc("ssq"), 1.0 / D, ALU.mult, EPS, ALU.add)
        rsqrt(stc("rstd"), stc("tmp")[:, 0:1])
        act(hb.ap(), src, AF.Identity, scale=stc("rstd"))
        bk = bank()
        bv = bk.ap().bitcast(BF16)
        pe([("tr", bv[:, k * 128:(k + 1) * 128], hb[:, k * 128:(k + 1) * 128], identb.ap()) for k in range(8)],
           r=[hb, identb], w=[bk])
        cp("dve", dstT[:, :, dst_cols], bv.rearrange("p (k t) -> p k t", k=8))
        rel(bk)

    def seq_prologue(sq):
        for mc in range(2):
            s = stage(mem_d[sq * MEM + mc * 128: sq * MEM + (mc + 1) * 128, :])
            norm_transpose(s.ap(), mhT, slice(mc * 128, (mc + 1) * 128), stc)
            rel_stg(s)
        for pair in range(2):
            bk = bank()
            pe([("mm", bk[:, 0:256], wm[:, k, pair * 128:(pair + 1) * 128], mhT[:, k, :], k == 0, k == 7)
                for k in range(8)], r=[wm, mhT], w=[bk])
            ts(kmT[:, pair, :], bk[:, 0:256], 0.125, ALU.mult)
            rel(bk)
        memset("dve", vme.ap(), 1.0)
        for mc in range(2):
            bk = bank()
            pe([("mm", bk[:, 0:256], mhT[:, k, mc * 128:(mc + 1) * 128], wm[:, k, 256:512], k == 0, k == 7)
                for k in range(8)], r=[wm, mhT], w=[bk])
            cp("dve", vme[:, mc, :, 0:64], bk[:, 0:256].rearrange("p (h d) -> p h d", h=4))
            rel(bk)
        memset("dve", S32.ap(), 0.0)
        memset("dve", Sbf.ap(), 0.0)
        memset("dve", R32.ap(), 0.0)
        memset("dve", Rbf.ap(), 0.0)

    cur_xs = [None]
    BSTEP = 1

    def tile_body(sq, T):
        ti = tile_idx[0]
        tile_idx[0] += 1
        pp = ti % 2
        stc = lambda name: stb[pp][name].ap()
        v2, kd, ke, qkT, gz = v2_[pp], kd_[pp], ke_[pp], qkT_[pp], gz_[pp]
        qkb, vb, vbz, qcT = qkb_[pp], vb_[pp], vbz_[pp], qcT_[pp]
        row0 = sq * S + T * 128
        xs, rp = pending.pop(0)
        cur_xs[0] = xs
        if ti + 1 < len(order):
            prefetch(ti + 1)
        yield
        pc, pcprev = pcT[ti % 2], pcT[(ti + 1) % 2]
        norm_transpose(xs.ap(), hT, slice(0, 128), stc)
        yield
        bz, bqk, bvz, bxz = bank(), bank(), bank(), bank()
        for bk, c0, n in ((bz, 0, 512), (bqk, 512, 512), (bvz, 1024, 512), (bxz, 1536, 264)):
            pe([("mm", bk[:, 0:n], hT[:, k, :], wtok[:, k, c0:c0 + n], k == 0, k == 7) for k in range(8)],
               r=[hT, wtok], w=[bk])
            yield
        yield
        for bk, n, c0 in ((bz, 512, 0), (bvz, 256, 512), (bxz, 256, 768)):
            src = bk[:, 0:n] if bk is not bvz else bk[:, 256:512]
            act(th[:, 0:n], src, AF.Tanh, scale=0.5)
            stt(gz[:, c0:c0 + n], th[:, 0:n], 1.0, src, ALU.add, ALU.mult)
        yield
        tt("dve", stc("xa"), bxz[:, 260:264], sm2[:, 4:8], ALU.add)
        act(stc("tb"), bxz[:, 256:260], AF.Tanh, scale=0.5)
        act(stc("e"), stc("xa"), AF.Exp)
        act(stc("sp"), stc("e"), AF.Ln, bias=1.0)
        tt("dve", stc("g"), stc("sp"), negA.ap(), ALU.mult)
        ts(stc("hb"), stc("tb"), 0.25, ALU.mult, 0.25, ALU.add)
        ts(stc("nb"), stc("tb"), -0.5, ALU.mult, -0.5, ALU.add)
        yield
        cp("act", qkb.ap(), bqk.ap())
        cp("act", vb.ap(), bvz[:, 0:256])
        tt("dve", vbz.ap().rearrange("p (h d) -> p h d", h=4), vb.ap().rearrange("p (h d) -> p h d", h=4),
           C("zeta").unsqueeze(2).to_broadcast([128, 4, 64]), ALU.mult)
        rel(bz, bqk, bvz, bxz)
        yield
        bq = bank()
        for pair in range(2):
            pe([("mm", bq[:, pair * 128:(pair + 1) * 128], wf[:, k, 1536 + pair * 128:1536 + (pair + 1) * 128],
                 hT[:, k, :], k == 0, k == 7) for k in range(8)], r=[hT, wf], w=[bq])
        for h2 in range(2):
            ts(qcT[:, h2 * 2:h2 * 2 + 2, :], bq[:, 0:256].rearrange("p (a t) -> p a t", a=2), C("hm")[:, h2:h2 + 1], ALU.mult)
        rel(bq)
        yield
        if T == 0:
            memset("dve", pc[:, :, 0:3], 0.0)
        else:
            cp("dve", pc[:, :, 0:3], pcprev[:, :, 128:131])
        bcv = []
        for grp in range(3):
            bk = bank()
            for j in range(4):
                cch = grp * 4 + j
                pe([("mm", bk[:, j * 128:(j + 1) * 128], wf[:, k, cch * 128:(cch + 1) * 128], hT[:, k, :], k == 0, k == 7)
                    for k in range(8)], r=[hT, wf], w=[bk])
                if j % 2 == 1:
                    yield
            cp("act" if grp != 1 else "dve", pc[:, grp * 4:(grp + 1) * 4, 3:131], bk.ap().rearrange("p (j t) -> p j t", j=4))
            rel(bk)
        for grp in range(3):
            bk = bank()
            for j in range(4):
                cch = grp * 4 + j
                pe([("mm", bk[:, j * 128:(j + 1) * 128], pc[:, cch, t:t + 128], cdiag[:, cch * 4 + t, :], t == 0, t == 3)
                    for t in range(4)], r=[pc, cdiag], w=[bk])
            bcv.append(bk)
            yield
        for grp in range(3):
            bk = bcv[grp]
            act(th.ap(), bk.ap(), AF.Tanh, scale=0.5)
            dst = q2k2[:, grp * 512:(grp + 1) * 512] if grp < 2 else v2.ap()
            stt(dst, th.ap(), 1.0, bk.ap(), ALU.add, ALU.mult)
            rel(bk)
            yield
        yield
        for j in range(8):
            act(junkF[:, 0:128], q2k2[:, j * 128:(j + 1) * 128], AF.Square, accum=stc("ss8")[:, j:j + 1])
        ts(stc("ss8"), stc("ss8"), 4.0 * EPS, ALU.add)
        rsqrt(stc("r8"), stc("ss8"))
        ts(stc("r8")[:, 0:4], stc("r8")[:, 0:4], 128.0 ** -0.5, ALU.mult)
        yield
        bg = bank()
        pe([("mm", bg[:, 0:4], C("ule"), stc("g"), True, True),
            ("mm", bg[:, 4:8], C("ones"), stc("g"), True, True)], r=[cst, stc("g")], w=[bg])
        cp("dve", stc("G"), bg[:, 0:4])
        act(stc("eG"), bg[:, 0:4], AF.Exp)
        act(stc("eGl"), bg[:, 4:8], AF.Exp)
        tt("dve", stc("dG"), bg[:, 4:8], stc("G"), ALU.subtract)
        act(stc("eGd"), stc("dG"), AF.Exp)
        rel(bg)
        tt("dve", stc("sq"), stc("r8")[:, 0:4], stc("eG"), ALU.mult)
        tt("dve", stc("skd"), stc("r8")[:, 4:8], stc("eGd"), ALU.mult)
        tt("dve", stc("ske"), stc("r8")[:, 4:8], stc("eG"), ALU.mult)

        def bc4(a):
            return a.unsqueeze(2).to_broadcast([128, 4, 128])

        def v3(a):
            return a.rearrange("p (h d) -> p h d", h=4)
        q2, k2 = q2k2[:, 0:512], q2k2[:, 512:1024]
        tt("dve", v3(qh.ap()), v3(q2), bc4(stc("r8")[:, 0:4]), ALU.mult)
        tt("dve", v3(kh.ap()), v3(k2), bc4(stc("r8")[:, 4:8]), ALU.mult)
        tt("dve", v3(qt.ap()), v3(q2), bc4(stc("sq")), ALU.mult)
        tt("dve", v3(kd.ap()), v3(k2), bc4(stc("skd")), ALU.mult)
        tt("dve", v3(ke.ap()), v3(k2), bc4(stc("ske")), ALU.mult)
        yield
        for i, srcb in enumerate((qh, kh, qt)):
            bk = bank()
            bv = bk.ap().bitcast(BF16)
            pe([("tr", bv[:, h * 128:(h + 1) * 128], srcb[:, h * 128:(h + 1) * 128], identb.ap()) for h in range(4)],
               r=[srcb, identb], w=[bk])
            cp("act" if i != 1 else "dve", qkT[:, i * 4:(i + 1) * 4, :], bv[:, 0:512].rearrange("p (h t) -> p h t", h=4))
            rel(bk)
            yield
        yield 'MID'
        tt("dve", Rg.ap(), C("ule").unsqueeze(1).to_broadcast([128, 4, 128]), bc4(stc("g")), ALU.mult)
        bd = bank()
        pe([("mm", bd.ap(), C("mgt"), Rg.ap().rearrange("p h c -> p (h c)"), True, True)], r=[cst, Rg], w=[bd])
        act(gts.ap(), bd.ap(), AF.Exp)
        rel(bd)
        tt("dve", v3(gti.ap()), v3(gts.ap()), C("mui").unsqueeze(1).to_broadcast([128, 4, 128]), ALU.mult)
        tt("dve", v3(gts.ap()), v3(gti.ap()), C("mus").unsqueeze(1).to_broadcast([128, 4, 128]), ALU.mult)
        yield
        bkk, bqkT = bank(), bank()
        pe([("mm", bkk[:, h * 128:(h + 1) * 128], qkT[:, 4 + h, :], qkT[:, 4 + h, :], True, True) for h in range(4)],
           r=[qkT], w=[bkk])
        pe([("mm", bqkT[:, h * 128:(h + 1) * 128], qkT[:, 4 + h, :], qkT[:, h, :], True, True) for h in range(4)],
           r=[qkT], w=[bqkT])
        A0 = AP_[0]
        for h in range(4):
            stt(A0[:, h, 0, :], bkk[:, h * 128:(h + 1) * 128], stc("nb")[:, h:h + 1], gts[:, h * 128:(h + 1) * 128],
                ALU.mult, ALU.mult)
        tt("dve", PT.ap(), v3(bqkT.ap()), v3(gti.ap()), ALU.mult)
        rel(bkk, bqkT)
        tt("dve", A0[:, :, 1, :], A0[:, :, 0, :], C("ident").unsqueeze(1).to_broadcast([128, 4, 128]), ALU.add)
        bb = bank()
        pe([("tr", bb[:, h * 128:(h + 1) * 128], A0[:, h, 0, :], C("ident")) for h in range(4)], r=[A0, cst], w=[bb])
        cp("act", Bm[0].ap(), v3(bb.ap()))
        rel(bb)
        yield
        ba = bank()
        pe([("mm", ba[:, h * 128:(h + 1) * 128], Bm[0][:, h, :], A0[:, h, 0, :], True, True) for h in range(4)],
           r=[Bm[0], A0], w=[ba])
        bb = bank()
        pe([("mm", bb[:, h * 128:(h + 1) * 128], A0[:, h, 0, :], Bm[0][:, h, :], True, True) for h in range(4)],
           r=[Bm[0], A0], w=[bb])
        A1 = AP_[1]
        cp("act", A1[:, :, 0, :], v3(ba.ap()))
        cp("dve", A1[:, :, 1, :], A0[:, :, 1, :])
        cp("act", Bm[1].ap(), v3(bb.ap()))
        rel(ba, bb)
        cur = 1
        for lvl in range(1, 7):
            Ak, Bk = AP_[cur], Bm[cur]
            An, Bn = AP_[1 - cur], Bm[1 - cur]
            last = lvl == 6
            b1, b2 = bank(), bank()
            for hp in range(2):
                bk = (b1, b2)[hp]
                items = []
                for hh in range(2):
                    h = hp * 2 + hh
                    if last:
                        items.append(("mm", bk[:, hh * 256 + 128: hh * 256 + 256], Bk[:, h, :], Ak[:, h, 1, :], True, True))
                    else:
                        items.append(("mm", bk[:, hh * 256:(hh + 1) * 256], Bk[:, h, :],
                                      Ak[:, h, :, :].rearrange("p a c -> p (a c)"), True, True))
                pe(items, r=[Bk, Ak], w=[bk])
            if not last:
                b3 = bank()
                pe([("mm", b3[:, h * 128:(h + 1) * 128], Ak[:, h, 0, :], Bk[:, h, :], True, True) for h in range(4)],
                   r=[Bk, Ak], w=[b3])
            for hp in range(2):
                bk = (b1, b2)[hp]
                bv = bk.ap().rearrange("p (h a c) -> p h a c", h=2, a=2)
                if last:
                    tt("dve", TT[:, hp * 2:hp * 2 + 2, :], Ak[:, hp * 2:hp * 2 + 2, 1, :], bv[:, :, 1, :], ALU.add)
                else:
                    cp("act", An[:, hp * 2:hp * 2 + 2, 0, :], bv[:, :, 0, :])
                    tt("dve", An[:, hp * 2:hp * 2 + 2, 1, :], Ak[:, hp * 2:hp * 2 + 2, 1, :], bv[:, :, 1, :], ALU.add)
            rel(b1, b2)
            if not last:
                cp("act", Bn.ap(), v3(b3.ap()))
                rel(b3)
            cur = 1 - cur
        yield
        bu_ps, bw_ps = bank(), bank()
        pe([("mm", bu_ps[:, h * 128:(h + 1) * 128], TT[:, h, :], v2[:, h * 128:(h + 1) * 128], True, True) for h in range(4)],
           r=[TT, v2], w=[bu_ps])
        pe([("mm", bw_ps[:, h * 128:(h + 1) * 128], ke[:, h * 128:(h + 1) * 128], TT[:, h, :], True, True) for h in range(4)],
           r=[TT, ke], w=[bw_ps])
        yield
        tt("dve", v3(bu.ap()), v3(bu_ps.ap()), bc4(stc("hb")), ALU.mult)
        cp("act", wT.ap(), v3(bw_ps.ap()))
        rel(bu_ps, bw_ps)
        yield
        bws, bo, bds = bank(), bank(), bank()
        pe([("mm", bws[:, h * 128:(h + 1) * 128], wT[:, h, :], Sbf[:, h, :], True, True) for h in range(4)],
           r=[wT, Sbf], w=[bws])
        for h in range(4):
            hs = slice(h * 128, (h + 1) * 128)
            stt(vn[:, hs], bws[:, hs], stc("nb")[:, h:h + 1], bu[:, hs], ALU.mult, ALU.add)
        rel(bws)
        yield
        for h in range(4):
            hs = slice(h * 128, (h + 1) * 128)
            pe([("mm", bo[:, hs], qkT[:, 8 + h, :], Sbf[:, h, :], True, False),
                ("mm", bo[:, hs], PT[:, h, :], vn[:, hs], False, True)], r=[qkT, Sbf, PT, vn], w=[bo])
        pe([("mm", bds[:, h * 128:(h + 1) * 128], kd[:, h * 128:(h + 1) * 128], vn[:, h * 128:(h + 1) * 128], True, True)
            for h in range(4)], r=[kd, vn], w=[bds])
        for h in range(4):
            hs = slice(h * 128, (h + 1) * 128)
            stt(S32[:, h, :], S32[:, h, :], stc("eGl")[:, h:h + 1], bds[:, hs], ALU.mult, ALU.add)
        cp("act", Sbf.ap(), S32.ap())
        rel(bds)
        yield
        for h in range(4):
            act(junkB[:, 0:128], bo[:, h * 128:(h + 1) * 128], AF.Square, accum=stc("so")[:, h:h + 1])
        ts(stc("so"), stc("so"), 1.0 / 128.0, ALU.mult, EPS, ALU.add)
        rsqrt(stc("ro"), stc("so"))
        for h in range(4):
            hs = slice(h * 128, (h + 1) * 128)
            stt(y[:, hs], bo[:, hs], stc("ro")[:, h:h + 1], gz[:, hs], ALU.mult, ALU.mult)
        rel(bo)

        yield
        rv = lambda a: a.rearrange("p (g i two) -> p g i two", g=8, two=2)
        cosb = rp[:, 0:64].unsqueeze(1).to_broadcast([128, 8, 64])
        tt("dve", r1.ap().rearrange("p (g d) -> p g d", g=8), qkb.ap().rearrange("p (g d) -> p g d", g=8), cosb, ALU.mult)
        nse = rp[:, 64:96].unsqueeze(1).to_broadcast([128, 8, 32])
        so_ = rp[:, 96:128].unsqueeze(1).to_broadcast([128, 8, 32])
        tt("dve", rv(r2.ap())[:, :, :, 0], rv(qkb.ap())[:, :, :, 1], nse, ALU.mult)
        tt("dve", rv(r2.ap())[:, :, :, 1], rv(qkb.ap())[:, :, :, 0], so_, ALU.mult)
        tt("dve", qkr.ap(), r1.ap(), r2.ap(), ALU.add)
        yield
        bk = bank()
        bv = bk.ap().bitcast(BF16)
        pe([("tr", bv[:, j * 128:(j + 1) * 128], qkr[:, j * 128:(j + 1) * 128], identb.ap()) for j in range(4)],
           r=[qkr, identb], w=[bk])
        cp("act", rT[:, 0:2, :], bv[:, 256:512].rearrange("p (j t) -> p j t", j=2))
        for h2 in range(2):
            ts(rT[:, 2 + h2 * 2:4 + h2 * 2, :], bv[:, 0:256].rearrange("p (j t) -> p j t", j=2), C("hm")[:, h2:h2 + 1], ALU.mult)
            tt("dve", rT[:, 6 + h2 * 2:8 + h2 * 2, :], bv[:, 0:256].rearrange("p (j t) -> p j t", j=2),
               C("xi%d" % h2).rearrange("p (j t) -> p j t", j=2), ALU.mult)
        rel(bk)
        yield
        bsc = bank()
        items = []
        for h in range(4):
            pr, h2 = h // 2, h % 2
            prt = slice(h2 * 64, (h2 + 1) * 64)
            items.append(("mm", bsc[:, h * 128:(h + 1) * 128], rT[:, pr, :], rT[:, 2 + h2 * 2 + pr, :], True, True))
        pe(items, r=[rT], w=[bsc])
        tt("dve", PbT.ap(), v3(bsc.ap()), v3(C("dmat")), ALU.mult)
        rel(bsc)
        yield
        bob = bank()
        for h in range(4):
            pr, h2 = h // 2, h % 2
            prt = slice(h2 * 64, (h2 + 1) * 64)
            pe([("mm", bob[:, h * 64:(h + 1) * 64], PbT[:, h, :], vb[:, h * 64:(h + 1) * 64], True, False),
                ("mm", bob[:, h * 64:(h + 1) * 64], rT[:, 6 + h2 * 2 + pr, :], Rbf[:, pr, :], False, True)],
               r=[PbT, vb, rT, Rbf], w=[bob])
        yield
        bkv = bank()
        pe([("mm", bkv[:, pr * 128:(pr + 1) * 128], qkr[:, 256 + pr * 128:256 + (pr + 1) * 128],
             vbz[:, pr * 128:(pr + 1) * 128], True, True) for pr in range(2)], r=[qkr, vbz], w=[bkv])
        yield
        for pr in range(2):
            for h2 in range(2):
                prt = slice(h2 * 64, (h2 + 1) * 64)
                stt(R32[prt, pr, :], R32[prt, pr, :], C("dec")[prt, pr:pr + 1],
                    bkv[prt, pr * 128 + h2 * 64: pr * 128 + (h2 + 1) * 64], ALU.mult, ALU.add)
        cp("act", Rbf.ap(), R32.ap())
        rel(bkv)
        yield
        ob3 = bob[:, 0:256].rearrange("p (h d) -> p h d", h=4)
        tmp2 = stc("tmp2")
        P.op("dve", lambda e: e.tensor_reduce(out=_ap(tmp2), in_=_ap(ob3), axis=mybir.AxisListType.X, op=ALU.add),
             r=[bob], w=[tmp2])
        ts(tmp2, tmp2, -1.0 / 64.0, ALU.mult)
        tt("dve", ynb.ap().rearrange("p (h d) -> p h d", h=4), ob3,
           tmp2.unsqueeze(2).to_broadcast([128, 4, 64]), ALU.add)
        rel(bob)
        for h in range(4):
            act(junkB[:, 0:64], ynb[:, h * 64:(h + 1) * 64], AF.Square, accum=stc("so2")[:, h:h + 1])
        ts(stc("so2"), stc("so2"), 1.0 / 64.0, ALU.mult, EPS, ALU.add)
        rsqrt(stc("ro2"), stc("so2"))
        for h in range(4):
            hs = slice(h * 64, (h + 1) * 64)
            stt(y[:, 512 + h * 64:512 + (h + 1) * 64], ynb[:, hs], stc("ro2")[:, h:h + 1],
                gz[:, 512 + h * 64:512 + (h + 1) * 64], ALU.mult, ALU.mult)

        yield
        bl1, bl2 = bank(), bank()
        for hp in range(2):
            bk = (bl1, bl2)[hp]
            items = []
            for hh in range(2):
                h = hp * 2 + hh
                pr, h2 = h // 2, h % 2
                prt = slice(h2 * 64, (h2 + 1) * 64)
                for mc in range(2):
                    items.append(("mm", bk[:, (hh * 2 + mc) * 128:(hh * 2 + mc + 1) * 128],
                                  kmT[:, pr, mc * 128:(mc + 1) * 128], qcT[:, h2 * 2 + pr, :], True, True))
            pe(items, r=[kmT, qcT], w=[bk])
            act(pT[:, hp * 4:(hp + 1) * 4, :], bk.ap().rearrange("p (j t) -> p j t", j=4), AF.Exp)
            rel(bk)
            yield
        boc = bank()
        for h in range(4):
            pe([("mm", boc[:, h * 65:(h + 1) * 65], pT[:, h * 2 + mc, :], vme[:, mc, h, :], mc == 0, mc == 1)
                for mc in range(2)], r=[pT, vme], w=[boc])
        oc3 = boc[:, 0:260].rearrange("p (h d) -> p h d", h=4)
        rden = stc("rden")
        P.op("dve", lambda e: e.reciprocal(out=_ap(rden), in_=_ap(oc3[:, :, 64])), r=[boc], w=[rden])
        for h in range(4):
            stt(y[:, 768 + h * 64:768 + (h + 1) * 64], boc[:, h * 65:h * 65 + 64], stc("rden")[:, h:h + 1],
                gz[:, 768 + h * 64:768 + (h + 1) * 64], ALU.mult, ALU.mult)
        rel(boc)

        yield
        bk = bank()
        bv = bk.ap().bitcast(BF16)
        pe([("tr", bv[:, k * 128:(k + 1) * 128], y[:, k * 128:(k + 1) * 128], identb.ap()) for k in range(8)],
           r=[y, identb], w=[bk])
        cp("act", yT.ap(), bv.rearrange("p (k t) -> p k t", k=8))
        rel(bk)
        yield
        for half in range(2):
            bk = bank()
            pe([("mm", bk.ap(), yT[:, k, :], wo[:, k, half * 512:(half + 1) * 512], k == 0, k == 7) for k in range(8)],
               r=[yT, wo], w=[bk])
            tt("dve", xs[:, half * 512:(half + 1) * 512], bk.ap(), xs[:, half * 512:(half + 1) * 512], ALU.add)
            rel(bk)
        act(y.ap(), xs.ap(), AF.Square, accum=stc("ssq2"))
        ts(stc("tmp2")[:, 0:1], stc("ssq2"), 1.0 / D, ALU.mult, EPS, ALU.add)
        rsqrt(stc("rstd2"), stc("tmp2")[:, 0:1])
        stt(xs.ap(), xs.ap(), stc("rstd2"), fw.ap(), ALU.mult, ALU.mult)
        last_store[xs.name] = P.dma(out_d[row0:row0 + 128, :], xs.ap())
        rel_stg(xs)

    order = [(sq, T) for sq in range(NSEQ) for T in range(NT)]
    pending = []
    last_store = {}

    def prefetch(i):
        sq_, T_ = order[i]
        r0 = sq_ * S + T_ * 128
        xs_ = stage(x_d[r0:r0 + 128, :])
        rp_ = rope[i % 4]
        P.dma(rp_.ap(), rope_d[T_])
        pending.append((xs_, rp_))

    def adv(g):
        try:
            return next(g)
        except StopIteration:
            return "END"

    PIPE = True
    for sq in range(NSEQ):
        seq_prologue(sq)
        if sq == 0:
            prefetch(0)
        prev = None
        for T in range(NT):
            g = tile_body(sq, T)
            fdone, bdone = False, prev is None
            while not (fdone and bdone):
                if not bdone:
                    for _ in range(BSTEP):
                        if adv(prev) == "END":
                            bdone = True
                            break
                if not fdone:
                    if adv(g) == "MID":
                        fdone = True
            if PIPE:
                prev = g
            else:
                while adv(g) != "END":
                    pass
                prev = None
        if prev is not None:
            while adv(prev) != "END":
                pass
    P.finish_wait("sp", list(last_store.values()))
    import os as _os
    if _os.environ.get("KSTATS"):
        print("op counts", P.cnt, {e: len(v) for e, v in P.stream.items()},
              {e: sum(len(w[0]) for w in v) for e, v in P.stream.items()}, flush=True)
    with nc.allow_non_contiguous_dma(reason="tiny param loads"):
        P.emit()
    es.close()
    return nc, cat, rope_np


_CACHE = {}


def _get(NSEQ, S):
    key = (NSEQ, S)
    if key not in _CACHE:
        _CACHE[key] = build(NSEQ, S)
    return _CACHE[key]


def run(inputs, NSEQ, S, ncores):
    nc, cat, rope_np = _get(NSEQ, S)
    f = lambda a: np.ascontiguousarray(np.asarray(a, dtype=np.float32))
    x = f(inputs["x"]); mem = f(inputs["mem"])
    shared = {
        "w_in": f(inputs["w_in"][0]), "w_out": f(inputs["w_out"][0]), "w_mem_kv": f(inputs["w_mem_kv"][0]),
        "norm_w": f(inputs["norm_w"][0]), "mem_norm_w": f(inputs["mem_norm_w"][0]),
        "gdn_conv_w": f(inputs["gdn_conv_w"][0]), "gdn_A_log": f(inputs["gdn_A_log"][0]),
        "gdn_dt_bias": f(inputs["gdn_dt_bias"][0]), "gdn_norm_w": f(inputs["gdn_norm_w"][0]),
        "ret_gn_w": f(inputs["ret_gn_w"][0]), "final_norm_w": f(inputs["final_norm_w"]),
        "cst": cat, "rope": rope_np,
    }
    in_maps = []
    for c in range(ncores):
        m = dict(shared)
        m["x"] = np.ascontiguousarray(x[c * NSEQ:(c + 1) * NSEQ].reshape(NSEQ * S, D))
        m["mem"] = np.ascontiguousarray(mem[c * NSEQ:(c + 1) * NSEQ].reshape(NSEQ * MEM, D))
        in_maps.append(m)
    res = run_bass_kernel_spmd(nc, in_maps, core_ids=list(range(ncores)))
    outs = [np.asarray(r["out"]).reshape(NSEQ, S, D) for r in res.results]
    return np.concatenate(outs, axis=0).astype(np.float32)


def kernel(**inputs):
    B, S, _ = inputs["x"].shape
    NSEQ = B // NCORES
    return run(inputs, NSEQ, S, NCORES)
```

```python
import numpy as np
from contextlib import ExitStack
import concourse.bass as bass
import concourse.mybir as mybir
from concourse.bass_utils import run_bass_kernel_spmd

F32 = mybir.dt.float32
BF16 = mybir.dt.bfloat16
AF = mybir.ActivationFunctionType
ALU = mybir.AluOpType

D = 1024
NCORES = 8
EPS = 1e-6
IN_COLS = 3592
MEM = 256


class Buf:
    def __init__(self, prog, t, name, dma_sem=None):
        self.prog, self.t, self.name = prog, t, name
        self.last_w = None
        self.readers = []
        self.dma_sem = dma_sem
        self.dma_cnt = 0

    def __getitem__(self, k):
        return TAP(self, self.t[k])

    def ap(self):
        return TAP(self, self.t[:])


class TAP:
    def __init__(self, buf, ap):
        self.buf, self.ap = buf, ap

    def __getitem__(self, k):
        return TAP(self.buf, self.ap[k])

    def rearrange(self, s, **kw):
        return TAP(self.buf, self.ap.rearrange(s, **kw))

    def unsqueeze(self, a):
        return TAP(self.buf, self.ap.unsqueeze(a))

    def to_broadcast(self, shp):
        return TAP(self.buf, self.ap.to_broadcast(shp))

    def bitcast(self, dt):
        return TAP(self.buf, self.ap.bitcast(dt))


def _ap(x):
    return x.ap if isinstance(x, TAP) else x


def _bufs(xs):
    out = []
    for x in xs:
        if isinstance(x, TAP):
            if x.buf not in out:
                out.append(x.buf)
        elif isinstance(x, Buf):
            if x not in out:
                out.append(x)
    return out


class Prog:
    ENGS = ("pe", "act", "dve", "pool", "sp")

    def __init__(self, nc, es):
        self.nc, self.es = nc, es
        self.sems = {e: es.enter_context(nc.semaphore("s_" + e)) for e in self.ENGS}
        self.cnt = {e: 0 for e in self.ENGS}
        self.seen = {e: {} for e in self.ENGS}
        self.stream = {e: [] for e in self.ENGS}
        self.nbuf = 0

    def sb(self, name, shape, dt, dma=False):
        t = self.es.enter_context(self.nc.sbuf_tensor("t_" + name, list(shape), dt))
        ds = self.es.enter_context(self.nc.semaphore("d_" + name)) if dma else None
        return Buf(self, t, name, ds)

    def ps(self, name, shape, dt):
        t = self.es.enter_context(self.nc.psum_tensor("p_" + name, list(shape), dt))
        return Buf(self, t, name)

    def op(self, eng, fn, r=(), w=()):
        reads, writes = _bufs(r), _bufs(w)
        deps = []
        for b in reads:
            if b.last_w is not None:
                deps.append(b.last_w)
        for b in writes:
            if b.last_w is not None:
                deps.append(b.last_w)
            for rr in b.readers:
                deps.append(rr)
        need = {}
        for (sk, val, e2) in deps:
            if e2 == eng and eng == "pe":
                continue
            if self.seen[eng].get(id(sk), 0) < val:
                if need.get(id(sk), (None, 0))[1] < val:
                    need[id(sk)] = (sk, val)
        waits = list(need.values())
        for sk, val in waits:
            self.seen[eng][id(sk)] = val
        self.cnt[eng] += 1
        ev = (self.sems[eng], self.cnt[eng], eng)
        self.stream[eng].append((waits, fn, self.sems[eng], 1))
        for b in writes:
            b.last_w = ev
            b.readers = []
        for b in reads:
            if b not in writes:
                b.readers.append(ev)
        return ev

    def dma(self, out, in_, eng="sp"):
        ob = out.buf if isinstance(out, TAP) else None
        ib = in_.buf if isinstance(in_, TAP) else None
        tb = ob if ob is not None else ib
        assert tb is not None and tb.dma_sem is not None, "dma buffer needs dma=True"
        reads = [ib] if ib is not None else []
        writes = [ob] if ob is not None else []
        deps = []
        for b in reads:
            if b.last_w is not None:
                deps.append(b.last_w)
        for b in writes:
            if b.last_w is not None:
                deps.append(b.last_w)
            deps.extend(b.readers)
        need = {}
        for (sk, val, e2) in deps:
            if self.seen[eng].get(id(sk), 0) < val:
                if need.get(id(sk), (None, 0))[1] < val:
                    need[id(sk)] = (sk, val)
        waits = list(need.values())
        for sk, val in waits:
            self.seen[eng][id(sk)] = val
        tb.dma_cnt += 16
        ev = (tb.dma_sem, tb.dma_cnt, "dma")
        oa, ia = _ap(out), _ap(in_)
        self.stream[eng].append((waits, lambda e: e.dma_start(out=oa, in_=ia), tb.dma_sem, 16))
        for b in writes:
            b.last_w = ev
            b.readers = []
        for b in reads:
            b.readers.append(ev)
        return ev

    def finish_wait(self, eng, evs):
        self.stream[eng].append(([(sk, val) for (sk, val, _) in evs], None, None, 0))

    def emit(self):
        nc = self.nc
        block = self.es.enter_context(nc.Block())
        semeng = {id(self.sems[e]): e for e in self.ENGS}
        needed = {e: set() for e in self.ENGS}
        for e in self.ENGS:
            for waits, fn, sem, inc in self.stream[e]:
                for sk, val in waits:
                    if id(sk) in semeng:
                        needed[semeng[id(sk)]].add(val)
        incmap = {}
        for e in self.ENGS:
            m, c, seq = {}, 0, 0
            for waits, fn, sem, inc in self.stream[e]:
                if fn is None or inc != 1:
                    continue
                seq += 1
                if seq in needed[e]:
                    c += 1
                m[seq] = c
            incmap[e] = m

        def replay(name):
            def body(e):
                seq = 0
                for waits, fn, sem, inc in self.stream[name]:
                    for sk, val in waits:
                        if id(sk) in semeng:
                            e.wait_ge(sk, incmap[semeng[id(sk)]][val])
                        else:
                            e.wait_ge(sk, val)
                    if fn is not None:
                        ins = fn(e)
                        if inc == 1:
                            seq += 1
                            if seq in needed[name]:
                                ins.then_inc(sem, 1)
                        else:
                            ins.then_inc(sem, inc)
            return body

        block.tensor(replay("pe"))
        block.scalar(replay("act"))
        block.vector(replay("dve"))
        block.gpsimd(replay("pool"))
        block.sync(replay("sp"))


def _consts(S):
    NT = S // 128
    p = np.arange(128)
    c = {}
    c["ident"] = np.eye(128, dtype=np.float32)
    c["ule"] = (p[:, None] <= p[None, :]).astype(np.float32)
    c["mgt"] = (p[:, None] > p[None, :]).astype(np.float32)
    c["ones"] = np.ones((128, 128), np.float32)
    c["mus"] = (p[None, :] > p[:, None]).astype(np.float32)
    c["mui"] = (p[None, :] >= p[:, None]).astype(np.float32)
    H = 4
    lg = np.log(np.float32(1.0) - np.float32(2.0) ** (-5.0 - np.arange(H, dtype=np.float32))).astype(np.float32)
    diff = (p[None, :] - p[:, None]).astype(np.float32)
    dm = np.zeros((128, H, 128), np.float32)
    for h in range(H):
        dm[:, h, :] = np.where(diff >= 0, np.exp(lg[h] * np.maximum(diff, 0.0)), 0.0) * 0.125
    c["dmat"] = dm.reshape(128, 512)
    zeta = np.exp(lg[None, :] * (127.0 - p[:, None].astype(np.float32))).astype(np.float32) * 0.125
    c["zeta"] = zeta
    xi = np.zeros((128, 2, 128), np.float32)
    dec = np.zeros((128, 2), np.float32)
    for pair in range(2):
        for h2 in range(2):
            h = pair * 2 + h2
            xi[h2 * 64:(h2 + 1) * 64, pair, :] = np.exp(lg[h] * (p.astype(np.float32) + 1.0))[None, :]
            dec[h2 * 64:(h2 + 1) * 64, pair] = np.exp(lg[h] * 128.0)
    xi0 = xi.copy(); xi0[64:128] = 0.0
    xi1 = xi.copy(); xi1[0:64] = 0.0
    c["xi0"] = xi0.reshape(128, 256)
    c["xi1"] = xi1.reshape(128, 256)
    hm = np.zeros((128, 2), np.float32); hm[0:64, 0] = 1.0; hm[64:128, 1] = 1.0
    c["hm"] = hm
    c["dec"] = dec
    c["mhalf"] = np.full((128, 8), -0.5, np.float32)
    names = ["ident", "ule", "mgt", "ones", "mus", "mui", "dmat", "zeta", "xi0", "xi1", "hm", "dec", "mhalf"]
    offs, o = {}, 0
    for n in names:
        offs[n] = (o, c[n].shape[1])
        o += c[n].shape[1]
    cat = np.concatenate([c[n] for n in names], axis=1).astype(np.float32)
    ang = (1.0 / (np.float32(10000.0) ** np.linspace(0.0, 1.0, 32, dtype=np.float32))).astype(np.float32)
    ang = np.repeat(ang, 2)
    pos = np.arange(S, dtype=np.float32)
    theta = (pos[:, None] * ang[None, :]).astype(np.float32)
    cos = np.cos(theta.astype(np.float64)).astype(np.float32)
    sin = np.sin(theta.astype(np.float64)).astype(np.float32)
    rope = np.concatenate([cos, -sin[:, 0::2], sin[:, 1::2]], axis=1).reshape(NT, 128, 128).astype(np.float32)
    return cat, offs, np.ascontiguousarray(rope)


def build(NSEQ, S):
    NT = S // 128
    cat, offs, rope_np = _consts(S)
    NC_ = cat.shape[1]
    nc = bass.Bass("TRN2", target_bir_lowering=False)
    dram = lambda n, s, k: nc.dram_tensor(n, list(s), F32, kind=k).ap()
    x_d = dram("x", [NSEQ * S, D], "ExternalInput")
    mem_d = dram("mem", [NSEQ * MEM, D], "ExternalInput")
    win_d = dram("w_in", [D, IN_COLS], "ExternalInput")
    wout_d = dram("w_out", [D, D], "ExternalInput")
    wmem_d = dram("w_mem_kv", [D, 512], "ExternalInput")
    normw_d = dram("norm_w", [D], "ExternalInput")
    mnormw_d = dram("mem_norm_w", [D], "ExternalInput")
    convw_d = dram("gdn_conv_w", [4, 1536], "ExternalInput")
    alog_d = dram("gdn_A_log", [4], "ExternalInput")
    dtb_d = dram("gdn_dt_bias", [4], "ExternalInput")
    gnw_d = dram("gdn_norm_w", [128], "ExternalInput")
    rgw_d = dram("ret_gn_w", [256], "ExternalInput")
    fnw_d = dram("final_norm_w", [D], "ExternalInput")
    cst_d = dram("cst", [128, NC_], "ExternalInput")
    rope_d = dram("rope", [NT, 128, 128], "ExternalInput")
    out_d = dram("out", [NSEQ * S, D], "ExternalOutput")

    es = ExitStack()
    P = Prog(nc, es)
    sb, ps = P.sb, P.ps

    cst = sb("cst", [128, NC_], F32, dma=True)
    def C(name):
        o, n = offs[name]
        return cst[:, o:o + n]
    wtok = sb("wtok", [128, 8, 1800], BF16)
    wf = sb("wf", [128, 8, 1792], BF16)
    wo = sb("wo", [128, 8, 1024], BF16)
    wm = sb("wm", [128, 8, 512], BF16)
    cdiag = sb("cdiag", [128, 48, 128], BF16)
    identb = sb("identb", [128, 128], BF16)
    fw = sb("fw", [128, 1024], F32, dma=True)
    sm = sb("sm", [128, 64], F32, dma=True)
    sm2 = sb("sm2", [128, 16], F32, dma=True)
    rs = sb("rs", [128, 8], F32)
    negA = sb("negA", [128, 4], F32)
    stg = [sb("stg%d" % i, [128, 1024], F32, dma=True) for i in range(4)]
    rope = [sb("rope%d" % i, [128, 128], F32, dma=True) for i in range(4)]

    junkF = sb("junkF", [128, 128], BF16)
    junkB = sb("junkB", [128, 128], BF16)
    hb = sb("hb", [128, 1024], BF16)
    hT = sb("hT", [128, 8, 128], BF16)
    pcT = [sb("pcT%d" % i, [128, 12, 131], BF16) for i in range(2)]
    th = sb("th", [128, 512], F32)
    q2k2 = sb("q2k2", [128, 1024], BF16)
    v2_ = [sb("v2_%d" % i, [128, 512], BF16) for i in range(2)]
    qh = sb("qh", [128, 512], BF16)
    kh = sb("kh", [128, 512], BF16)
    qt = sb("qt", [128, 512], BF16)
    kd_ = [sb("kd_%d" % i, [128, 512], BF16) for i in range(2)]
    ke_ = [sb("ke_%d" % i, [128, 512], BF16) for i in range(2)]
    qkT_ = [sb("qkT_%d" % i, [128, 12, 128], BF16) for i in range(2)]
    Rg = sb("Rg", [128, 4, 128], F32)
    gts = sb("gts", [128, 512], F32)
    gti = sb("gti", [128, 512], F32)
    APp = [[sb("AP%d_%d" % (i, hp), [128, 2, 2, 128], F32) for hp in range(2)] for i in range(2)]
    Bmp = [[sb("Bm%d_%d" % (i, hp), [128, 2, 128], F32) for hp in range(2)] for i in range(2)]
    TT = sb("TT", [128, 4, 128], BF16)
    PT = sb("PT", [128, 4, 128], BF16)
    bu = sb("bu", [128, 512], BF16)
    wT = sb("wT", [128, 4, 128], BF16)
    vn = sb("vn", [128, 512], BF16)
    S32 = sb("S32", [128, 4, 128], F32)
    Sbf = sb("Sbf", [128, 4, 128], BF16)
    qkb_ = [sb("qkb_%d" % i, [128, 512], BF16) for i in range(2)]
    r1 = sb("r1", [128, 512], BF16)
    r2 = sb("r2", [128, 512], BF16)
    qkr = sb("qkr", [128, 512], BF16)
    rT = sb("rT", [128, 10, 128], BF16)
    PbT = sb("PbT", [128, 4, 128], BF16)
    vb_ = [sb("vb_%d" % i, [128, 256], BF16) for i in range(2)]
    vbz_ = [sb("vbz_%d" % i, [128, 256], BF16) for i in range(2)]
    R32 = sb("R32", [128, 2, 64], F32)
    Rbf = sb("Rbf", [128, 2, 64], BF16)
    ynb = sb("ynb", [128, 256], F32)
    qcT_ = [sb("qcT_%d" % i, [128, 4, 128], BF16) for i in range(2)]
    pT = sb("pT", [128, 8, 128], BF16)
    kmT = sb("kmT", [128, 2, 256], BF16)
    vme = sb("vme", [128, 2, 4, 65], BF16)
    gz_ = [sb("gz_%d" % i, [128, 1024], BF16) for i in range(2)]
    y = sb("y", [128, 1024], BF16)
    yT = sb("yT", [128, 8, 128], BF16)
    banks = [ps("bank%d" % i, [128, 512], F32) for i in range(8)]
    bkc = [0]

    free_banks = list(banks)

    def bank():
        assert free_banks, "out of PSUM banks"
        return free_banks.pop(0)

    def rel(*bs):
        for b in bs:
            assert b not in free_banks
            free_banks.append(b)

    STN = dict(ssq=1, rstd=1, ss8=8, r8=8, xa=4, e=4, sp=4, g=4, tb=4, hb=4, nb=4, G=4, eG=4, eGl=4, dG=4,
               eGd=4, sq=4, skd=4, ske=4, so=4, ro=4, so2=4, ro2=4, rden=4, ssq2=1, rstd2=1, tmp=4, tmp2=4)
    stb = [{k: sb("st%d_%s" % (i, k), [128, n], F32) for k, n in STN.items()} for i in range(2)]
    par = [0]

    def stc(name):
        return stb[0][name].ap()

    mhalf = C("mhalf")

    def act(out, in_, func, scale=1.0, bias=None, accum=None, extra_r=()):
        kw = {}
        if bias is not None:
            kw["bias"] = _ap(bias)
        if accum is not None:
            kw["accum_out"] = _ap(accum)
        sc = _ap(scale)
        P.op("act", lambda e: e.activation(out=_ap(out), in_=_ap(in_), func=func, scale=sc, **kw),
             r=[in_, scale, bias] + list(extra_r), w=[out, accum])

    def tt(eng, out, in0, in1, op):
        P.op(eng, lambda e: e.tensor_tensor(out=_ap(out), in0=_ap(in0), in1=_ap(in1), op=op), r=[in0, in1], w=[out])

    def ts(out, in0, s1, op0, s2=None, op1=None, eng="dve"):
        if op1 is None:
            P.op(eng, lambda e: e.tensor_scalar(out=_ap(out), in0=_ap(in0), scalar1=_ap(s1), scalar2=None, op0=op0),
                 r=[in0, s1], w=[out])
        else:
            P.op(eng, lambda e: e.tensor_scalar(out=_ap(out), in0=_ap(in0), scalar1=_ap(s1), scalar2=_ap(s2),
                                               op0=op0, op1=op1), r=[in0, s1, s2], w=[out])

    def stt(out, in0, scalar, in1, op0, op1, eng="dve"):
        P.op(eng, lambda e: e.scalar_tensor_tensor(out=_ap(out), in0=_ap(in0), scalar=_ap(scalar), in1=_ap(in1),
                                                   op0=op0, op1=op1), r=[in0, scalar, in1], w=[out])

    def cp(eng, out, in_):
        if eng == "act":
            P.op("act", lambda e: e.copy(out=_ap(out), in_=_ap(in_)), r=[in_], w=[out])
        else:
            P.op(eng, lambda e: e.tensor_copy(out=_ap(out), in_=_ap(in_)), r=[in_], w=[out])

    def memset(eng, out, val):
        P.op(eng, lambda e: e.memset(_ap(out), val), r=[], w=[out])

    def pe(items, r, w):
        def fn(e):
            ins = None
            for it in items:
                if it[0] == "mm":
                    ins = e.matmul(_ap(it[1]), lhsT=_ap(it[2]), rhs=_ap(it[3]), start=it[4], stop=it[5])
                else:
                    ins = e.transpose(out=_ap(it[1]), in_=_ap(it[2]), identity=_ap(it[3]))
            return ins
        P.op("pe", fn, r=r, w=w)

    def rsqrt(out, in_):
        n = in_.ap.shape[-1]
        tt("pool", out, in_, mhalf[:, 0:n], ALU.pow)

    P.dma(cst.ap(), cst_d)
    P.dma(fw.ap(), fnw_d.partition_broadcast(128))
    with nc.allow_non_contiguous_dma(reason="tiny param loads"):
        P.dma(sm[:, 0:8], normw_d.rearrange("(k p) -> p k", p=128))
        P.dma(sm[:, 8:16], mnormw_d.rearrange("(k p) -> p k", p=128))
        for t in range(4):
            P.dma(sm[:, 16 + t * 12:16 + (t + 1) * 12], convw_d[t].rearrange("(c p) -> p c", p=128))
        P.dma(sm2[:, 0:1], gnw_d.rearrange("(p o) -> p o", o=1))
        P.dma(sm2[:, 1:3], rgw_d.rearrange("(k p) -> p k", p=128))
        P.dma(sm2[:, 4:8], dtb_d.partition_broadcast(128))
        P.dma(sm2[:, 8:12], alog_d.partition_broadcast(128))
    cp("dve", identb.ap(), C("ident"))
    act(negA.ap(), sm2[:, 8:12], AF.Exp)
    ts(negA.ap(), negA.ap(), -1.0, ALU.mult)
    for k in range(4):
        ts(rs[:, k:k + 1], sm2[:, 0:1], 0.5, ALU.mult)
    ts(rs[:, 4:6], sm2[:, 1:3], 0.5, ALU.mult)
    memset("dve", rs[:, 6:8], 0.5)
    for cch in range(12):
        for t in range(4):
            ts(cdiag[:, cch * 4 + t, :], identb.ap(), sm[:, 16 + t * 12 + cch:17 + t * 12 + cch], ALU.mult)
    free_stg = list(stg)

    def stage(src_ap):
        assert free_stg, "out of staging slots"
        s = free_stg.pop(0)
        P.dma(s[:, 0:src_ap.shape[1]], src_ap)
        return s

    def rel_stg(s):
        assert s not in free_stg
        free_stg.append(s)

    for k in range(8):
        rows = slice(k * 128, (k + 1) * 128)
        nwk = sm[:, k:k + 1]
        s = stage(win_d[rows, 0:1024]);    ts(wf[:, k, 0:1024], s[:, 0:1024], nwk, ALU.mult); rel_stg(s)
        s = stage(win_d[rows, 1024:2048])
        ts(wf[:, k, 1024:1536], s[:, 0:512], nwk, ALU.mult)
        ts(wtok[:, k, 0:512], s[:, 512:1024], nwk, ALU.mult); rel_stg(s)
        s = stage(win_d[rows, 2048:3072])
        ts(wtok[:, k, 1792:1800], s[:, 0:8], nwk, ALU.mult)
        ts(wtok[:, k, 512:1528], s[:, 8:1024], nwk, ALU.mult); rel_stg(s)
        s = stage(win_d[rows, 3072:3592])
        ts(wtok[:, k, 1528:1536], s[:, 0:8], nwk, ALU.mult)
        ts(wf[:, k, 1536:1792], s[:, 8:264], nwk, ALU.mult)
        ts(wtok[:, k, 1536:1792], s[:, 264:520], nwk, ALU.mult); rel_stg(s)
        s = stage(wout_d[rows, :]);        ts(wo[:, k, :], s[:, 0:1024], rs[:, k:k + 1], ALU.mult); rel_stg(s)
        s = stage(wmem_d[rows, :]);        ts(wm[:, k, :], s[:, 0:512], sm[:, 8 + k:9 + k], ALU.mult); rel_stg(s)

    tile_idx = [0]

    def norm_transpose(src, dstT, dst_cols, stc):
        act(hb.ap(), src, AF.Square, accum=stc("ssq"))
        ts(stc("tmp")[:, 0:1], stc("ssq"), 1.0 / D, ALU.mult, EPS, ALU.add)
        rsqrt(stc("rstd"), stc("tmp")[:, 0:1])
        act(hb.ap(), src, AF.Identity, scale=stc("rstd"))
        bk = bank()
        bv = bk.ap().bitcast(BF16)
        pe([("tr", bv[:, k * 128:(k + 1) * 128], hb[:, k * 128:(k + 1) * 128], identb.ap()) for k in range(8)],
           r=[hb, identb], w=[bk])
        cp("dve", dstT[:, :, dst_cols], bv.rearrange("p (k t) -> p k t", k=8))
        rel(bk)

    def seq_prologue(sq):
        mslot = free_stg.pop(0)
        mhT = mslot.ap().bitcast(BF16).rearrange("p (k m) -> p k m", k=8)
        for mc in range(2):
            s = stage(mem_d[sq * MEM + mc * 128: sq * MEM + (mc + 1) * 128, :])
            norm_transpose(s.ap(), mhT, slice(mc * 128, (mc + 1) * 128), stc)
            rel_stg(s)
        for pair in range(2):
            bk = bank()
            pe([("mm", bk[:, 0:256], wm[:, k, pair * 128:(pair + 1) * 128], mhT[:, k, :], k == 0, k == 7)
                for k in range(8)], r=[wm, mhT], w=[bk])
            ts(kmT[:, pair, :], bk[:, 0:256], 0.125, ALU.mult)
            rel(bk)
        memset("dve", vme.ap(), 1.0)
        for mc in range(2):
            bk = bank()
            pe([("mm", bk[:, 0:256], mhT[:, k, mc * 128:(mc + 1) * 128], wm[:, k, 256:512], k == 0, k == 7)
                for k in range(8)], r=[wm, mhT], w=[bk])
            cp("dve", vme[:, mc, :, 0:64], bk[:, 0:256].rearrange("p (h d) -> p h d", h=4))
            rel(bk)
        rel_stg(mslot)
        memset("dve", S32.ap(), 0.0)
        memset("dve", Sbf.ap(), 0.0)
        memset("dve", R32.ap(), 0.0)
        memset("dve", Rbf.ap(), 0.0)

    cur_xs = [None]
    BSTEP = 1

    def tile_body(sq, T):
        ti = tile_idx[0]
        tile_idx[0] += 1
        pp = ti % 2
        stc = lambda name: stb[pp][name].ap()
        v2, kd, ke, qkT, gz = v2_[pp], kd_[pp], ke_[pp], qkT_[pp], gz_[pp]
        qkb, vb, vbz, qcT = qkb_[pp], vb_[pp], vbz_[pp], qcT_[pp]
        row0 = sq * S + T * 128
        xs, rp = pending.pop(0)
        cur_xs[0] = xs
        if ti + 1 < len(order):
            prefetch(ti + 1)
        yield
        pc, pcprev = pcT[ti % 2], pcT[(ti + 1) % 2]
        norm_transpose(xs.ap(), hT, slice(0, 128), stc)
        yield
        bz, bqk, bvz, bxz = bank(), bank(), bank(), bank()
        for bk, c0, n in ((bz, 0, 512), (bqk, 512, 512), (bvz, 1024, 512), (bxz, 1536, 264)):
            pe([("mm", bk[:, 0:n], hT[:, k, :], wtok[:, k, c0:c0 + n], k == 0, k == 7) for k in range(8)],
               r=[hT, wtok], w=[bk])
            yield
        yield
        for bk, n, c0 in ((bz, 512, 0), (bvz, 256, 512), (bxz, 256, 768)):
            src = bk[:, 0:n] if bk is not bvz else bk[:, 256:512]
            act(th[:, 0:n], src, AF.Tanh, scale=0.5)
            stt(gz[:, c0:c0 + n], th[:, 0:n], 1.0, src, ALU.add, ALU.mult)
        yield
        tt("dve", stc("xa"), bxz[:, 260:264], sm2[:, 4:8], ALU.add)
        act(stc("tb"), bxz[:, 256:260], AF.Tanh, scale=0.5)
        act(stc("e"), stc("xa"), AF.Exp)
        act(stc("sp"), stc("e"), AF.Ln, bias=1.0)
        tt("dve", stc("g"), stc("sp"), negA.ap(), ALU.mult)
        ts(stc("hb"), stc("tb"), 0.25, ALU.mult, 0.25, ALU.add)
        ts(stc("nb"), stc("tb"), -0.5, ALU.mult, -0.5, ALU.add)
        yield
        cp("act", qkb.ap(), bqk.ap())
        cp("act", vb.ap(), bvz[:, 0:256])
        tt("dve", vbz.ap().rearrange("p (h d) -> p h d", h=4), vb.ap().rearrange("p (h d) -> p h d", h=4),
           C("zeta").unsqueeze(2).to_broadcast([128, 4, 64]), ALU.mult)
        rel(bz, bqk, bvz, bxz)
        yield
        bq = bank()
        for pair in range(2):
            pe([("mm", bq[:, pair * 128:(pair + 1) * 128], wf[:, k, 1536 + pair * 128:1536 + (pair + 1) * 128],
                 hT[:, k, :], k == 0, k == 7) for k in range(8)], r=[hT, wf], w=[bq])
        for h2 in range(2):
            ts(qcT[:, h2 * 2:h2 * 2 + 2, :], bq[:, 0:256].rearrange("p (a t) -> p a t", a=2), C("hm")[:, h2:h2 + 1], ALU.mult)
        rel(bq)
        yield
        if T == 0:
            memset("dve", pc[:, :, 0:3], 0.0)
        else:
            cp("dve", pc[:, :, 0:3], pcprev[:, :, 128:131])
        bcv = []
        for grp in range(3):
            bk = bank()
            for j in range(4):
                cch = grp * 4 + j
                pe([("mm", bk[:, j * 128:(j + 1) * 128], wf[:, k, cch * 128:(cch + 1) * 128], hT[:, k, :], k == 0, k == 7)
                    for k in range(8)], r=[hT, wf], w=[bk])
                if j % 2 == 1:
                    yield
            cp("act" if grp != 1 else "dve", pc[:, grp * 4:(grp + 1) * 4, 3:131], bk.ap().rearrange("p (j t) -> p j t", j=4))
            rel(bk)
        for grp in range(3):
            bk = bank()
            for j in range(4):
                cch = grp * 4 + j
                pe([("mm", bk[:, j * 128:(j + 1) * 128], pc[:, cch, t:t + 128], cdiag[:, cch * 4 + t, :], t == 0, t == 3)
                    for t in range(4)], r=[pc, cdiag], w=[bk])
            bcv.append(bk)
            yield
        for grp in range(3):
            bk = bcv[grp]
            act(th.ap(), bk.ap(), AF.Tanh, scale=0.5)
            dst = q2k2[:, grp * 512:(grp + 1) * 512] if grp < 2 else v2.ap()
            stt(dst, th.ap(), 1.0, bk.ap(), ALU.add, ALU.mult)
            rel(bk)
            yield
        yield
        for j in range(8):
            act(junkF[:, 0:128], q2k2[:, j * 128:(j + 1) * 128], AF.Square, accum=stc("ss8")[:, j:j + 1])
        ts(stc("ss8"), stc("ss8"), 4.0 * EPS, ALU.add)
        rsqrt(stc("r8"), stc("ss8"))
        ts(stc("r8")[:, 0:4], stc("r8")[:, 0:4], 128.0 ** -0.5, ALU.mult)
        yield
        bg = bank()
        pe([("mm", bg[:, 0:4], C("ule"), stc("g"), True, True),
            ("mm", bg[:, 4:8], C("ones"), stc("g"), True, True)], r=[cst, stc("g")], w=[bg])
        cp("dve", stc("G"), bg[:, 0:4])
        act(stc("eG"), bg[:, 0:4], AF.Exp)
        act(stc("eGl"), bg[:, 4:8], AF.Exp)
        tt("dve", stc("dG"), bg[:, 4:8], stc("G"), ALU.subtract)
        act(stc("eGd"), stc("dG"), AF.Exp)
        rel(bg)
        tt("dve", stc("sq"), stc("r8")[:, 0:4], stc("eG"), ALU.mult)
        tt("dve", stc("skd"), stc("r8")[:, 4:8], stc("eGd"), ALU.mult)
        tt("dve", stc("ske"), stc("r8")[:, 4:8], stc("eG"), ALU.mult)

        def bc4(a):
            return a.unsqueeze(2).to_broadcast([128, 4, 128])

        def v3(a):
            return a.rearrange("p (h d) -> p h d", h=4)
        q2, k2 = q2k2[:, 0:512], q2k2[:, 512:1024]
        tt("dve", v3(qh.ap()), v3(q2), bc4(stc("r8")[:, 0:4]), ALU.mult)
        tt("dve", v3(kh.ap()), v3(k2), bc4(stc("r8")[:, 4:8]), ALU.mult)
        tt("dve", v3(qt.ap()), v3(q2), bc4(stc("sq")), ALU.mult)
        tt("dve", v3(kd.ap()), v3(k2), bc4(stc("skd")), ALU.mult)
        tt("dve", v3(ke.ap()), v3(k2), bc4(stc("ske")), ALU.mult)
        yield
        for i, srcb in enumerate((qh, kh, qt)):
            bk = bank()
            bv = bk.ap().bitcast(BF16)
            pe([("tr", bv[:, h * 128:(h + 1) * 128], srcb[:, h * 128:(h + 1) * 128], identb.ap()) for h in range(4)],
               r=[srcb, identb], w=[bk])
            cp("act" if i != 1 else "dve", qkT[:, i * 4:(i + 1) * 4, :], bv[:, 0:512].rearrange("p (h t) -> p h t", h=4))
            rel(bk)
            yield
        yield 'MID'
        tt("dve", Rg.ap(), C("ule").unsqueeze(1).to_broadcast([128, 4, 128]), bc4(stc("g")), ALU.mult)
        bd = bank()
        pe([("mm", bd.ap(), C("mgt"), Rg.ap().rearrange("p h c -> p (h c)"), True, True)], r=[cst, Rg], w=[bd])
        act(gts.ap(), bd.ap(), AF.Exp)
        rel(bd)
        tt("dve", v3(gti.ap()), v3(gts.ap()), C("mui").unsqueeze(1).to_broadcast([128, 4, 128]), ALU.mult)
        tt("dve", v3(gts.ap()), v3(gti.ap()), C("mus").unsqueeze(1).to_broadcast([128, 4, 128]), ALU.mult)
        yield
        bkk, bqkT = bank(), bank()
        pe([("mm", bkk[:, h * 128:(h + 1) * 128], qkT[:, 4 + h, :], qkT[:, 4 + h, :], True, True) for h in range(4)],
           r=[qkT], w=[bkk])
        pe([("mm", bqkT[:, h * 128:(h + 1) * 128], qkT[:, 4 + h, :], qkT[:, h, :], True, True) for h in range(4)],
           r=[qkT], w=[bqkT])
        identf = C("ident")
        for hp in range(2):
            A0 = APp[0][hp]
            for hh in range(2):
                h = hp * 2 + hh
                stt(A0[:, hh, 0, :], bkk[:, h * 128:(h + 1) * 128], stc("nb")[:, h:h + 1], gts[:, h * 128:(h + 1) * 128],
                    ALU.mult, ALU.mult)
            tt("dve", A0[:, :, 1, :], A0[:, :, 0, :], identf.unsqueeze(1).to_broadcast([128, 2, 128]), ALU.add)
        tt("dve", PT.ap(), v3(bqkT.ap()), v3(gti.ap()), ALU.mult)
        rel(bkk, bqkT)
        yield
        for hp in range(2):
            A0 = APp[0][hp]
            bb = bank()
            pe([("tr", bb[:, hh * 128:(hh + 1) * 128], A0[:, hh, 0, :], identf) for hh in range(2)], r=[A0, cst], w=[bb])
            cp("act", Bmp[0][hp].ap(), bb[:, 0:256].rearrange("p (h c) -> p h c", h=2))
            rel(bb)
        cur = 0
        for lvl in range(0, 7):
            last = lvl == 6
            for hp in range(2):
                Ak, Bk = APp[cur][hp], Bmp[cur][hp]
                An, Bn = APp[1 - cur][hp], Bmp[1 - cur][hp]
                if not last:
                    b3 = bank()
                    pe([("mm", b3[:, hh * 128:(hh + 1) * 128], Ak[:, hh, 0, :], Bk[:, hh, :], True, True) for hh in range(2)],
                       r=[Bk, Ak], w=[b3])
                    cp("act", Bn.ap(), b3[:, 0:256].rearrange("p (h c) -> p h c", h=2))
                    rel(b3)
                b1 = bank()
                if lvl == 0:
                    pe([("mm", b1[:, hh * 256:hh * 256 + 128], Bk[:, hh, :], Ak[:, hh, 0, :], True, True) for hh in range(2)],
                       r=[Bk, Ak], w=[b1])
                elif last:
                    pe([("mm", b1[:, hh * 256 + 128:hh * 256 + 256], Bk[:, hh, :], Ak[:, hh, 1, :], True, True) for hh in range(2)],
                       r=[Bk, Ak], w=[b1])
                else:
                    pe([("mm", b1[:, hh * 256:(hh + 1) * 256], Bk[:, hh, :], Ak[:, hh, :, :].rearrange("p a c -> p (a c)"),
                         True, True) for hh in range(2)], r=[Bk, Ak], w=[b1])
                bv = b1.ap().rearrange("p (h a c) -> p h a c", h=2, a=2)
                if lvl == 0:
                    cp("act", An[:, :, 0, :], bv[:, :, 0, :])
                    cp("dve", An[:, :, 1, :], Ak[:, :, 1, :])
                elif last:
                    tt("dve", TT[:, hp * 2:hp * 2 + 2, :], Ak[:, :, 1, :], bv[:, :, 1, :], ALU.add)
                else:
                    cp("act", An[:, :, 0, :], bv[:, :, 0, :])
                    tt("dve", An[:, :, 1, :], Ak[:, :, 1, :], bv[:, :, 1, :], ALU.add)
                rel(b1)
            cur = 1 - cur
        yield
        bu_ps, bw_ps = bank(), bank()
        pe([("mm", bu_ps[:, h * 128:(h + 1) * 128], TT[:, h, :], v2[:, h * 128:(h + 1) * 128], True, True) for h in range(4)],
           r=[TT, v2], w=[bu_ps])
        pe([("mm", bw_ps[:, h * 128:(h + 1) * 128], ke[:, h * 128:(h + 1) * 128], TT[:, h, :], True, True) for h in range(4)],
           r=[TT, ke], w=[bw_ps])
        yield
        tt("dve", v3(bu.ap()), v3(bu_ps.ap()), bc4(stc("hb")), ALU.mult)
        cp("act", wT.ap(), v3(bw_ps.ap()))
        rel(bu_ps, bw_ps)
        yield
        bws, bo, bds = bank(), bank(), bank()
        pe([("mm", bws[:, h * 128:(h + 1) * 128], wT[:, h, :], Sbf[:, h, :], True, True) for h in range(4)],
           r=[wT, Sbf], w=[bws])
        for h in range(4):
            hs = slice(h * 128, (h + 1) * 128)
            stt(vn[:, hs], bws[:, hs], stc("nb")[:, h:h + 1], bu[:, hs], ALU.mult, ALU.add)
        rel(bws)
        yield
        for h in range(4):
            hs = slice(h * 128, (h + 1) * 128)
            pe([("mm", bo[:, hs], qkT[:, 8 + h, :], Sbf[:, h, :], True, False),
                ("mm", bo[:, hs], PT[:, h, :], vn[:, hs], False, True)], r=[qkT, Sbf, PT, vn], w=[bo])
        pe([("mm", bds[:, h * 128:(h + 1) * 128], kd[:, h * 128:(h + 1) * 128], vn[:, h * 128:(h + 1) * 128], True, True)
            for h in range(4)], r=[kd, vn], w=[bds])
        for h in range(4):
            hs = slice(h * 128, (h + 1) * 128)
            stt(S32[:, h, :], S32[:, h, :], stc("eGl")[:, h:h + 1], bds[:, hs], ALU.mult, ALU.add)
        cp("act", Sbf.ap(), S32.ap())
        rel(bds)
        yield
        for h in range(4):
            act(junkB[:, 0:128], bo[:, h * 128:(h + 1) * 128], AF.Square, accum=stc("so")[:, h:h + 1])
        ts(stc("so"), stc("so"), 1.0 / 128.0, ALU.mult, EPS, ALU.add)
        rsqrt(stc("ro"), stc("so"))
        for h in range(4):
            hs = slice(h * 128, (h + 1) * 128)
            stt(y[:, hs], bo[:, hs], stc("ro")[:, h:h + 1], gz[:, hs], ALU.mult, ALU.mult)
        rel(bo)

        yield
        rv = lambda a: a.rearrange("p (g i two) -> p g i two", g=8, two=2)
        cosb = rp[:, 0:64].unsqueeze(1).to_broadcast([128, 8, 64])
        tt("dve", r1.ap().rearrange("p (g d) -> p g d", g=8), qkb.ap().rearrange("p (g d) -> p g d", g=8), cosb, ALU.mult)
        nse = rp[:, 64:96].unsqueeze(1).to_broadcast([128, 8, 32])
        so_ = rp[:, 96:128].unsqueeze(1).to_broadcast([128, 8, 32])
        tt("dve", rv(r2.ap())[:, :, :, 0], rv(qkb.ap())[:, :, :, 1], nse, ALU.mult)
        tt("dve", rv(r2.ap())[:, :, :, 1], rv(qkb.ap())[:, :, :, 0], so_, ALU.mult)
        tt("dve", qkr.ap(), r1.ap(), r2.ap(), ALU.add)
        yield
        bk = bank()
        bv = bk.ap().bitcast(BF16)
        pe([("tr", bv[:, j * 128:(j + 1) * 128], qkr[:, j * 128:(j + 1) * 128], identb.ap()) for j in range(4)],
           r=[qkr, identb], w=[bk])
        cp("act", rT[:, 0:2, :], bv[:, 256:512].rearrange("p (j t) -> p j t", j=2))
        for h2 in range(2):
            ts(rT[:, 2 + h2 * 2:4 + h2 * 2, :], bv[:, 0:256].rearrange("p (j t) -> p j t", j=2), C("hm")[:, h2:h2 + 1], ALU.mult)
            tt("dve", rT[:, 6 + h2 * 2:8 + h2 * 2, :], bv[:, 0:256].rearrange("p (j t) -> p j t", j=2),
               C("xi%d" % h2).rearrange("p (j t) -> p j t", j=2), ALU.mult)
        rel(bk)
        yield
        bsc = bank()
        items = []
        for h in range(4):
            pr, h2 = h // 2, h % 2
            prt = slice(h2 * 64, (h2 + 1) * 64)
            items.append(("mm", bsc[:, h * 128:(h + 1) * 128], rT[:, pr, :], rT[:, 2 + h2 * 2 + pr, :], True, True))
        pe(items, r=[rT], w=[bsc])
        tt("dve", PbT.ap(), v3(bsc.ap()), v3(C("dmat")), ALU.mult)
        rel(bsc)
        yield
        bob = bank()
        for h in range(4):
            pr, h2 = h // 2, h % 2
            prt = slice(h2 * 64, (h2 + 1) * 64)
            pe([("mm", bob[:, h * 64:(h + 1) * 64], PbT[:, h, :], vb[:, h * 64:(h + 1) * 64], True, False),
                ("mm", bob[:, h * 64:(h + 1) * 64], rT[:, 6 + h2 * 2 + pr, :], Rbf[:, pr, :], False, True)],
               r=[PbT, vb, rT, Rbf], w=[bob])
        yield
        bkv = bank()
        pe([("mm", bkv[:, pr * 128:(pr + 1) * 128], qkr[:, 256 + pr * 128:256 + (pr + 1) * 128],
             vbz[:, pr * 128:(pr + 1) * 128], True, True) for pr in range(2)], r=[qkr, vbz], w=[bkv])
        yield
        for pr in range(2):
            for h2 in range(2):
                prt = slice(h2 * 64, (h2 + 1) * 64)
                stt(R32[prt, pr, :], R32[prt, pr, :], C("dec")[prt, pr:pr + 1],
                    bkv[prt, pr * 128 + h2 * 64: pr * 128 + (h2 + 1) * 64], ALU.mult, ALU.add)
        cp("act", Rbf.ap(), R32.ap())
        rel(bkv)
        yield
        ob3 = bob[:, 0:256].rearrange("p (h d) -> p h d", h=4)
        tmp2 = stc("tmp2")
        P.op("dve", lambda e: e.tensor_reduce(out=_ap(tmp2), in_=_ap(ob3), axis=mybir.AxisListType.X, op=ALU.add),
             r=[bob], w=[tmp2])
        ts(tmp2, tmp2, -1.0 / 64.0, ALU.mult)
        tt("dve", ynb.ap().rearrange("p (h d) -> p h d", h=4), ob3,
           tmp2.unsqueeze(2).to_broadcast([128, 4, 64]), ALU.add)
        rel(bob)
        for h in range(4):
            act(junkB[:, 0:64], ynb[:, h * 64:(h + 1) * 64], AF.Square, accum=stc("so2")[:, h:h + 1])
        ts(stc("so2"), stc("so2"), 1.0 / 64.0, ALU.mult, EPS, ALU.add)
        rsqrt(stc("ro2"), stc("so2"))
        for h in range(4):
            hs = slice(h * 64, (h + 1) * 64)
            stt(y[:, 512 + h * 64:512 + (h + 1) * 64], ynb[:, hs], stc("ro2")[:, h:h + 1],
                gz[:, 512 + h * 64:512 + (h + 1) * 64], ALU.mult, ALU.mult)

        yield
        bl1, bl2 = bank(), bank()
        for hp in range(2):
            bk = (bl1, bl2)[hp]
            items = []
            for hh in range(2):
                h = hp * 2 + hh
                pr, h2 = h // 2, h % 2
                prt = slice(h2 * 64, (h2 + 1) * 64)
                for mc in range(2):
                    items.append(("mm", bk[:, (hh * 2 + mc) * 128:(hh * 2 + mc + 1) * 128],
                                  kmT[:, pr, mc * 128:(mc + 1) * 128], qcT[:, h2 * 2 + pr, :], True, True))
            pe(items, r=[kmT, qcT], w=[bk])
            act(pT[:, hp * 4:(hp + 1) * 4, :], bk.ap().rearrange("p (j t) -> p j t", j=4), AF.Exp)
            rel(bk)
            yield
        boc = bank()
        for h in range(4):
            pe([("mm", boc[:, h * 65:(h + 1) * 65], pT[:, h * 2 + mc, :], vme[:, mc, h, :], mc == 0, mc == 1)
                for mc in range(2)], r=[pT, vme], w=[boc])
        oc3 = boc[:, 0:260].rearrange("p (h d) -> p h d", h=4)
        rden = stc("rden")
        P.op("dve", lambda e: e.reciprocal(out=_ap(rden), in_=_ap(oc3[:, :, 64])), r=[boc], w=[rden])
        for h in range(4):
            stt(y[:, 768 + h * 64:768 + (h + 1) * 64], boc[:, h * 65:h * 65 + 64], stc("rden")[:, h:h + 1],
                gz[:, 768 + h * 64:768 + (h + 1) * 64], ALU.mult, ALU.mult)
        rel(boc)

        yield
        bk = bank()
        bv = bk.ap().bitcast(BF16)
        pe([("tr", bv[:, k * 128:(k + 1) * 128], y[:, k * 128:(k + 1) * 128], identb.ap()) for k in range(8)],
           r=[y, identb], w=[bk])
        cp("act", yT.ap(), bv.rearrange("p (k t) -> p k t", k=8))
        rel(bk)
        yield
        for half in range(2):
            bk = bank()
            pe([("mm", bk.ap(), yT[:, k, :], wo[:, k, half * 512:(half + 1) * 512], k == 0, k == 7) for k in range(8)],
               r=[yT, wo], w=[bk])
            tt("dve", xs[:, half * 512:(half + 1) * 512], bk.ap(), xs[:, half * 512:(half + 1) * 512], ALU.add)
            rel(bk)
        act(y.ap(), xs.ap(), AF.Square, accum=stc("ssq2"))
        ts(stc("tmp2")[:, 0:1], stc("ssq2"), 1.0 / D, ALU.mult, EPS, ALU.add)
        rsqrt(stc("rstd2"), stc("tmp2")[:, 0:1])
        stt(xs.ap(), xs.ap(), stc("rstd2"), fw.ap(), ALU.mult, ALU.mult)
        last_store[xs.name] = P.dma(out_d[row0:row0 + 128, :], xs.ap())
        rel_stg(xs)

    order = [(sq, T) for sq in range(NSEQ) for T in range(NT)]
    pending = []
    last_store = {}

    def prefetch(i):
        sq_, T_ = order[i]
        r0 = sq_ * S + T_ * 128
        xs_ = stage(x_d[r0:r0 + 128, :])
        rp_ = rope[i % 4]
        P.dma(rp_.ap(), rope_d[T_])
        pending.append((xs_, rp_))

    def adv(g):
        try:
            return next(g)
        except StopIteration:
            return "END"

    PIPE = True
    for sq in range(NSEQ):
        seq_prologue(sq)
        if sq == 0:
            prefetch(0)
        prev = None
        for T in range(NT):
            g = tile_body(sq, T)
            fdone, bdone = False, prev is None
            while not (fdone and bdone):
                if not bdone:
                    for _ in range(BSTEP):
                        if adv(prev) == "END":
                            bdone = True
                            break
                if not fdone:
                    if adv(g) == "MID":
                        fdone = True
            if PIPE:
                prev = g
            else:
                while adv(g) != "END":
                    pass
                prev = None
        if prev is not None:
            while adv(prev) != "END":
                pass
    P.finish_wait("sp", list(last_store.values()))
    import os as _os
    if _os.environ.get("KSTATS"):
        print("op counts", P.cnt, {e: len(v) for e, v in P.stream.items()},
              {e: sum(len(w[0]) for w in v) for e, v in P.stream.items()}, flush=True)
    with nc.allow_non_contiguous_dma(reason="tiny param loads"):
        P.emit()
    es.close()
    return nc, cat, rope_np


_CACHE = {}


def _get(NSEQ, S):
    key = (NSEQ, S)
    if key not in _CACHE:
        _CACHE[key] = build(NSEQ, S)
    return _CACHE[key]


def run(inputs, NSEQ, S, ncores):
    nc, cat, rope_np = _get(NSEQ, S)
    f = lambda a: np.ascontiguousarray(np.asarray(a, dtype=np.float32))
    x = f(inputs["x"]); mem = f(inputs["mem"])
    shared = {
        "w_in": f(inputs["w_in"][0]), "w_out": f(inputs["w_out"][0]), "w_mem_kv": f(inputs["w_mem_kv"][0]),
        "norm_w": f(inputs["norm_w"][0]), "mem_norm_w": f(inputs["mem_norm_w"][0]),
        "gdn_conv_w": f(inputs["gdn_conv_w"][0]), "gdn_A_log": f(inputs["gdn_A_log"][0]),
        "gdn_dt_bias": f(inputs["gdn_dt_bias"][0]), "gdn_norm_w": f(inputs["gdn_norm_w"][0]),
        "ret_gn_w": f(inputs["ret_gn_w"][0]), "final_norm_w": f(inputs["final_norm_w"]),
        "cst": cat, "rope": rope_np,
    }
    in_maps = []
    for c in range(ncores):
        m = dict(shared)
        m["x"] = np.ascontiguousarray(x[c * NSEQ:(c + 1) * NSEQ].reshape(NSEQ * S, D))
        m["mem"] = np.ascontiguousarray(mem[c * NSEQ:(c + 1) * NSEQ].reshape(NSEQ * MEM, D))
        in_maps.append(m)
    res = run_bass_kernel_spmd(nc, in_maps, core_ids=list(range(ncores)))
    outs = [np.asarray(r["out"]).reshape(NSEQ, S, D) for r in res.results]
    return np.concatenate(outs, axis=0).astype(np.float32)


def kernel(**inputs):
    B, S, _ = inputs["x"].shape
    NSEQ = B // NCORES
    return run(inputs, NSEQ, S, NCORES)
```

```python
import numpy as np
from contextlib import ExitStack
import concourse.bass as bass
import concourse.mybir as mybir
from concourse.bass_utils import run_bass_kernel_spmd

F32 = mybir.dt.float32
BF16 = mybir.dt.bfloat16
AF = mybir.ActivationFunctionType
ALU = mybir.AluOpType

D = 1024
NCORES = 8
EPS = 1e-6
IN_COLS = 3592
MEM = 256


class Buf:
    def __init__(self, prog, t, name, dma_sem=None):
        self.prog, self.t, self.name = prog, t, name
        self.last_w = None
        self.readers = []
        self.dma_sem = dma_sem
        self.dma_cnt = 0

    def __getitem__(self, k):
        return TAP(self, self.t[k])

    def ap(self):
        return TAP(self, self.t[:])


class TAP:
    def __init__(self, buf, ap):
        self.buf, self.ap = buf, ap

    def __getitem__(self, k):
        return TAP(self.buf, self.ap[k])

    def rearrange(self, s, **kw):
        return TAP(self.buf, self.ap.rearrange(s, **kw))

    def unsqueeze(self, a):
        return TAP(self.buf, self.ap.unsqueeze(a))

    def to_broadcast(self, shp):
        return TAP(self.buf, self.ap.to_broadcast(shp))

    def bitcast(self, dt):
        return TAP(self.buf, self.ap.bitcast(dt))


def _ap(x):
    return x.ap if isinstance(x, TAP) else x


def _bufs(xs):
    out = []
    for x in xs:
        if isinstance(x, TAP):
            if x.buf not in out:
                out.append(x.buf)
        elif isinstance(x, Buf):
            if x not in out:
                out.append(x)
    return out


class Prog:
    ENGS = ("pe", "act", "dve", "pool", "sp")

    def __init__(self, nc, es):
        self.nc, self.es = nc, es
        self.sems = {e: es.enter_context(nc.semaphore("s_" + e)) for e in self.ENGS}
        self.cnt = {e: 0 for e in self.ENGS}
        self.seen = {e: {} for e in self.ENGS}
        self.stream = {e: [] for e in self.ENGS}
        self.nbuf = 0

    def sb(self, name, shape, dt, dma=False):
        t = self.es.enter_context(self.nc.sbuf_tensor("t_" + name, list(shape), dt))
        ds = self.es.enter_context(self.nc.semaphore("d_" + name)) if dma else None
        return Buf(self, t, name, ds)

    def ps(self, name, shape, dt):
        t = self.es.enter_context(self.nc.psum_tensor("p_" + name, list(shape), dt))
        return Buf(self, t, name)

    def op(self, eng, fn, r=(), w=()):
        reads, writes = _bufs(r), _bufs(w)
        deps = []
        for b in reads:
            if b.last_w is not None:
                deps.append(b.last_w)
        for b in writes:
            if b.last_w is not None:
                deps.append(b.last_w)
            for rr in b.readers:
                deps.append(rr)
        need = {}
        for (sk, val, e2) in deps:
            if e2 == eng and eng == "pe":
                continue
            if self.seen[eng].get(id(sk), 0) < val:
                if need.get(id(sk), (None, 0))[1] < val:
                    need[id(sk)] = (sk, val)
        waits = list(need.values())
        for sk, val in waits:
            self.seen[eng][id(sk)] = val
        self.cnt[eng] += 1
        ev = (self.sems[eng], self.cnt[eng], eng)
        self.stream[eng].append((waits, fn, self.sems[eng], 1))
        for b in writes:
            b.last_w = ev
            b.readers = []
        for b in reads:
            if b not in writes:
                b.readers.append(ev)
        return ev

    def dma(self, out, in_, eng="sp"):
        ob = out.buf if isinstance(out, TAP) else None
        ib = in_.buf if isinstance(in_, TAP) else None
        tb = ob if ob is not None else ib
        assert tb is not None and tb.dma_sem is not None, "dma buffer needs dma=True"
        reads = [ib] if ib is not None else []
        writes = [ob] if ob is not None else []
        deps = []
        for b in reads:
            if b.last_w is not None:
                deps.append(b.last_w)
        for b in writes:
            if b.last_w is not None:
                deps.append(b.last_w)
            deps.extend(b.readers)
        need = {}
        for (sk, val, e2) in deps:
            if self.seen[eng].get(id(sk), 0) < val:
                if need.get(id(sk), (None, 0))[1] < val:
                    need[id(sk)] = (sk, val)
        waits = list(need.values())
        for sk, val in waits:
            self.seen[eng][id(sk)] = val
        tb.dma_cnt += 16
        ev = (tb.dma_sem, tb.dma_cnt, "dma")
        oa, ia = _ap(out), _ap(in_)
        self.stream[eng].append((waits, lambda e: e.dma_start(out=oa, in_=ia), tb.dma_sem, 16))
        for b in writes:
            b.last_w = ev
            b.readers = []
        for b in reads:
            b.readers.append(ev)
        return ev

    def finish_wait(self, eng, evs):
        self.stream[eng].append(([(sk, val) for (sk, val, _) in evs], None, None, 0))

    def emit(self):
        nc = self.nc
        block = self.es.enter_context(nc.Block())
        semeng = {id(self.sems[e]): e for e in self.ENGS}
        needed = {e: set() for e in self.ENGS}
        for e in self.ENGS:
            for waits, fn, sem, inc in self.stream[e]:
                for sk, val in waits:
                    if id(sk) in semeng:
                        needed[semeng[id(sk)]].add(val)
        incmap = {}
        for e in self.ENGS:
            m, c, seq = {}, 0, 0
            for waits, fn, sem, inc in self.stream[e]:
                if fn is None or inc != 1:
                    continue
                seq += 1
                if seq in needed[e]:
                    c += 1
                m[seq] = c
            incmap[e] = m

        def replay(name):
            def body(e):
                seq = 0
                for waits, fn, sem, inc in self.stream[name]:
                    for sk, val in waits:
                        if id(sk) in semeng:
                            e.wait_ge(sk, incmap[semeng[id(sk)]][val])
                        else:
                            e.wait_ge(sk, val)
                    if fn is not None:
                        ins = fn(e)
                        if inc == 1:
                            seq += 1
                            if seq in needed[name]:
                                ins.then_inc(sem, 1)
                        else:
                            ins.then_inc(sem, inc)
            return body

        block.tensor(replay("pe"))
        block.scalar(replay("act"))
        block.vector(replay("dve"))
        block.gpsimd(replay("pool"))
        block.sync(replay("sp"))


def _consts(S):
    NT = S // 128
    p = np.arange(128)
    c = {}
    c["ident"] = np.eye(128, dtype=np.float32)
    c["ule"] = (p[:, None] <= p[None, :]).astype(np.float32)
    c["mgt"] = (p[:, None] > p[None, :]).astype(np.float32)
    c["ones"] = np.ones((128, 128), np.float32)
    c["mus"] = (p[None, :] > p[:, None]).astype(np.float32)
    c["mui"] = (p[None, :] >= p[:, None]).astype(np.float32)
    H = 4
    lg = np.log(np.float32(1.0) - np.float32(2.0) ** (-5.0 - np.arange(H, dtype=np.float32))).astype(np.float32)
    diff = (p[None, :] - p[:, None]).astype(np.float32)
    dm = np.zeros((128, H, 128), np.float32)
    for h in range(H):
        dm[:, h, :] = np.where(diff >= 0, np.exp(lg[h] * np.maximum(diff, 0.0)), 0.0) * 0.125
    c["dmat"] = dm.reshape(128, 512)
    zeta = np.exp(lg[None, :] * (127.0 - p[:, None].astype(np.float32))).astype(np.float32) * 0.125
    c["zeta"] = zeta
    xi = np.zeros((128, 2, 128), np.float32)
    dec = np.zeros((128, 2), np.float32)
    for pair in range(2):
        for h2 in range(2):
            h = pair * 2 + h2
            xi[h2 * 64:(h2 + 1) * 64, pair, :] = np.exp(lg[h] * (p.astype(np.float32) + 1.0))[None, :]
            dec[h2 * 64:(h2 + 1) * 64, pair] = np.exp(lg[h] * 128.0)
    xi0 = xi.copy(); xi0[64:128] = 0.0
    xi1 = xi.copy(); xi1[0:64] = 0.0
    c["xi0"] = xi0.reshape(128, 256)
    c["xi1"] = xi1.reshape(128, 256)
    hm = np.zeros((128, 2), np.float32); hm[0:64, 0] = 1.0; hm[64:128, 1] = 1.0
    c["hm"] = hm
    c["dec"] = dec
    c["mhalf"] = np.full((128, 8), -0.5, np.float32)
    names = ["ident", "ule", "mgt", "ones", "mus", "mui", "dmat", "zeta", "xi0", "xi1", "hm", "dec", "mhalf"]
    offs, o = {}, 0
    for n in names:
        offs[n] = (o, c[n].shape[1])
        o += c[n].shape[1]
    cat = np.concatenate([c[n] for n in names], axis=1).astype(np.float32)
    ang = (1.0 / (np.float32(10000.0) ** np.linspace(0.0, 1.0, 32, dtype=np.float32))).astype(np.float32)
    ang = np.repeat(ang, 2)
    pos = np.arange(S, dtype=np.float32)
    theta = (pos[:, None] * ang[None, :]).astype(np.float32)
    cos = np.cos(theta.astype(np.float64)).astype(np.float32)
    sin = np.sin(theta.astype(np.float64)).astype(np.float32)
    rope = np.concatenate([cos, -sin[:, 0::2], sin[:, 1::2]], axis=1).reshape(NT, 128, 128).astype(np.float32)
    return cat, offs, np.ascontiguousarray(rope)


def build(NSEQ, S):
    NT = S // 128
    cat, offs, rope_np = _consts(S)
    NC_ = cat.shape[1]
    nc = bass.Bass("TRN2", target_bir_lowering=False)
    dram = lambda n, s, k: nc.dram_tensor(n, list(s), F32, kind=k).ap()
    x_d = dram("x", [NSEQ * S, D], "ExternalInput")
    mem_d = dram("mem", [NSEQ * MEM, D], "ExternalInput")
    win_d = dram("w_in", [D, IN_COLS], "ExternalInput")
    wout_d = dram("w_out", [D, D], "ExternalInput")
    wmem_d = dram("w_mem_kv", [D, 512], "ExternalInput")
    normw_d = dram("norm_w", [D], "ExternalInput")
    mnormw_d = dram("mem_norm_w", [D], "ExternalInput")
    convw_d = dram("gdn_conv_w", [4, 1536], "ExternalInput")
    alog_d = dram("gdn_A_log", [4], "ExternalInput")
    dtb_d = dram("gdn_dt_bias", [4], "ExternalInput")
    gnw_d = dram("gdn_norm_w", [128], "ExternalInput")
    rgw_d = dram("ret_gn_w", [256], "ExternalInput")
    fnw_d = dram("final_norm_w", [D], "ExternalInput")
    cst_d = dram("cst", [128, NC_], "ExternalInput")
    rope_d = dram("rope", [NT, 128, 128], "ExternalInput")
    out_d = dram("out", [NSEQ * S, D], "ExternalOutput")

    es = ExitStack()
    P = Prog(nc, es)
    sb, ps = P.sb, P.ps

    cst = sb("cst", [128, NC_], F32, dma=True)
    def C(name):
        o, n = offs[name]
        return cst[:, o:o + n]
    wtok = sb("wtok", [128, 8, 1800], BF16)
    wf = sb("wf", [128, 8, 1792], BF16)
    wo = sb("wo", [128, 8, 1024], BF16)
    wm = sb("wm", [128, 8, 512], BF16)
    cdiag = sb("cdiag", [128, 48, 128], BF16)
    identb = sb("identb", [128, 128], BF16)
    fw = sb("fw", [128, 1024], F32, dma=True)
    sm = sb("sm", [128, 64], F32, dma=True)
    sm2 = sb("sm2", [128, 16], F32, dma=True)
    rs = sb("rs", [128, 8], F32)
    negA = sb("negA", [128, 4], F32)
    stg = [sb("stg%d" % i, [128, 1024], F32, dma=True) for i in range(4)]
    rope = [sb("rope%d" % i, [128, 128], F32, dma=True) for i in range(4)]

    junkF = sb("junkF", [128, 128], BF16)
    junkB = sb("junkB", [128, 128], BF16)
    hb = sb("hb", [128, 1024], BF16)
    hT = sb("hT", [128, 8, 128], BF16)
    pcT = [sb("pcT%d" % i, [128, 12, 131], BF16) for i in range(2)]
    th = sb("th", [128, 512], F32)
    q2k2 = sb("q2k2", [128, 1024], BF16)
    v2_ = [sb("v2_%d" % i, [128, 512], BF16) for i in range(2)]
    qh = sb("qh", [128, 512], BF16)
    kh = sb("kh", [128, 512], BF16)
    qt = sb("qt", [128, 512], BF16)
    kd_ = [sb("kd_%d" % i, [128, 512], BF16) for i in range(2)]
    ke_ = [sb("ke_%d" % i, [128, 512], BF16) for i in range(2)]
    qkT_ = [sb("qkT_%d" % i, [128, 12, 128], BF16) for i in range(2)]
    Rg = sb("Rg", [128, 4, 128], F32)
    gts = sb("gts", [128, 512], F32)
    gti = sb("gti", [128, 512], F32)
    APp = [[sb("AP%d_%d" % (i, hp), [128, 2, 2, 128], F32) for hp in range(2)] for i in range(2)]
    Bmp = [[sb("Bm%d_%d" % (i, hp), [128, 2, 128], F32) for hp in range(2)] for i in range(2)]
    TT = sb("TT", [128, 4, 128], BF16)
    PT = sb("PT", [128, 4, 128], BF16)
    bu = sb("bu", [128, 512], BF16)
    wT = sb("wT", [128, 4, 128], BF16)
    vn = sb("vn", [128, 512], BF16)
    S32 = sb("S32", [128, 4, 128], F32)
    Sbf = sb("Sbf", [128, 4, 128], BF16)
    qkb_ = [sb("qkb_%d" % i, [128, 512], BF16) for i in range(2)]
    r1 = sb("r1", [128, 512], BF16)
    r2 = sb("r2", [128, 512], BF16)
    qkr = sb("qkr", [128, 512], BF16)
    rT = sb("rT", [128, 10, 128], BF16)
    PbT = sb("PbT", [128, 4, 128], BF16)
    vb_ = [sb("vb_%d" % i, [128, 256], BF16) for i in range(2)]
    vbz_ = [sb("vbz_%d" % i, [128, 256], BF16) for i in range(2)]
    R32 = sb("R32", [128, 2, 64], F32)
    Rbf = sb("Rbf", [128, 2, 64], BF16)
    ynb = sb("ynb", [128, 256], F32)
    qcT_ = [sb("qcT_%d" % i, [128, 4, 128], BF16) for i in range(2)]
    pT = sb("pT", [128, 8, 128], BF16)
    kmT = sb("kmT", [128, 2, 256], BF16)
    vme = sb("vme", [128, 2, 4, 65], BF16)
    gz_ = [sb("gz_%d" % i, [128, 1024], BF16) for i in range(2)]
    y = sb("y", [128, 1024], BF16)
    yT = sb("yT", [128, 8, 128], BF16)
    banks = [ps("bank%d" % i, [128, 512], F32) for i in range(8)]
    bkc = [0]

    free_banks = list(banks)

    def bank():
        assert free_banks, "out of PSUM banks"
        return free_banks.pop(0)

    def rel(*bs):
        for b in bs:
            assert b not in free_banks
            free_banks.append(b)

    STN = dict(ssq=1, rstd=1, ss8=8, r8=8, xa=4, e=4, sp=4, g=4, tb=4, hb=4, nb=4, G=4, eG=4, eGl=4, dG=4,
               eGd=4, sq=4, skd=4, ske=4, so=4, ro=4, so2=4, ro2=4, rden=4, ssq2=1, rstd2=1, tmp=4, tmp2=4)
    stb = [{k: sb("st%d_%s" % (i, k), [128, n], F32) for k, n in STN.items()} for i in range(2)]
    par = [0]

    def stc(name):
        return stb[0][name].ap()

    mhalf = C("mhalf")

    def act(out, in_, func, scale=1.0, bias=None, accum=None, extra_r=()):
        kw = {}
        if bias is not None:
            kw["bias"] = _ap(bias)
        if accum is not None:
            kw["accum_out"] = _ap(accum)
        sc = _ap(scale)
        P.op("act", lambda e: e.activation(out=_ap(out), in_=_ap(in_), func=func, scale=sc, **kw),
             r=[in_, scale, bias] + list(extra_r), w=[out, accum])

    def tt(eng, out, in0, in1, op):
        P.op(eng, lambda e: e.tensor_tensor(out=_ap(out), in0=_ap(in0), in1=_ap(in1), op=op), r=[in0, in1], w=[out])

    def ts(out, in0, s1, op0, s2=None, op1=None, eng="dve"):
        if op1 is None:
            P.op(eng, lambda e: e.tensor_scalar(out=_ap(out), in0=_ap(in0), scalar1=_ap(s1), scalar2=None, op0=op0),
                 r=[in0, s1], w=[out])
        else:
            P.op(eng, lambda e: e.tensor_scalar(out=_ap(out), in0=_ap(in0), scalar1=_ap(s1), scalar2=_ap(s2),
                                               op0=op0, op1=op1), r=[in0, s1, s2], w=[out])

    def stt(out, in0, scalar, in1, op0, op1, eng="dve"):
        P.op(eng, lambda e: e.scalar_tensor_tensor(out=_ap(out), in0=_ap(in0), scalar=_ap(scalar), in1=_ap(in1),
                                                   op0=op0, op1=op1), r=[in0, scalar, in1], w=[out])

    def cp(eng, out, in_):
        if eng == "act":
            P.op("act", lambda e: e.copy(out=_ap(out), in_=_ap(in_)), r=[in_], w=[out])
        else:
            P.op(eng, lambda e: e.tensor_copy(out=_ap(out), in_=_ap(in_)), r=[in_], w=[out])

    def memset(eng, out, val):
        P.op(eng, lambda e: e.memset(_ap(out), val), r=[], w=[out])

    def pe(items, r, w):
        def fn(e):
            ins = None
            for it in items:
                if it[0] == "mm":
                    ins = e.matmul(_ap(it[1]), lhsT=_ap(it[2]), rhs=_ap(it[3]), start=it[4], stop=it[5])
                else:
                    ins = e.transpose(out=_ap(it[1]), in_=_ap(it[2]), identity=_ap(it[3]))
            return ins
        P.op("pe", fn, r=r, w=w)

    def rsqrt(out, in_):
        n = in_.ap.shape[-1]
        tt("pool", out, in_, mhalf[:, 0:n], ALU.pow)

    P.dma(cst.ap(), cst_d)
    P.dma(fw.ap(), fnw_d.partition_broadcast(128))
    with nc.allow_non_contiguous_dma(reason="tiny param loads"):
        P.dma(sm[:, 0:8], normw_d.rearrange("(k p) -> p k", p=128))
        P.dma(sm[:, 8:16], mnormw_d.rearrange("(k p) -> p k", p=128))
        for t in range(4):
            P.dma(sm[:, 16 + t * 12:16 + (t + 1) * 12], convw_d[t].rearrange("(c p) -> p c", p=128))
        P.dma(sm2[:, 0:1], gnw_d.rearrange("(p o) -> p o", o=1))
        P.dma(sm2[:, 1:3], rgw_d.rearrange("(k p) -> p k", p=128))
        P.dma(sm2[:, 4:8], dtb_d.partition_broadcast(128))
        P.dma(sm2[:, 8:12], alog_d.partition_broadcast(128))
    cp("dve", identb.ap(), C("ident"))
    act(negA.ap(), sm2[:, 8:12], AF.Exp)
    ts(negA.ap(), negA.ap(), -1.0, ALU.mult)
    for k in range(4):
        ts(rs[:, k:k + 1], sm2[:, 0:1], 0.5, ALU.mult)
    ts(rs[:, 4:6], sm2[:, 1:3], 0.5, ALU.mult)
    memset("dve", rs[:, 6:8], 0.5)
    for cch in range(12):
        for t in range(4):
            ts(cdiag[:, cch * 4 + t, :], identb.ap(), sm[:, 16 + t * 12 + cch:17 + t * 12 + cch], ALU.mult)
    free_stg = list(stg)

    def stage(src_ap):
        assert free_stg, "out of staging slots"
        s = free_stg.pop(0)
        P.dma(s[:, 0:src_ap.shape[1]], src_ap)
        return s

    def rel_stg(s):
        assert s not in free_stg
        free_stg.append(s)

    for k in range(8):
        rows = slice(k * 128, (k + 1) * 128)
        nwk = sm[:, k:k + 1]
        s = stage(win_d[rows, 0:1024]);    ts(wf[:, k, 0:1024], s[:, 0:1024], nwk, ALU.mult); rel_stg(s)
        s = stage(win_d[rows, 1024:2048])
        ts(wf[:, k, 1024:1536], s[:, 0:512], nwk, ALU.mult)
        ts(wtok[:, k, 0:512], s[:, 512:1024], nwk, ALU.mult); rel_stg(s)
        s = stage(win_d[rows, 2048:3072])
        ts(wtok[:, k, 1792:1800], s[:, 0:8], nwk, ALU.mult)
        ts(wtok[:, k, 512:1528], s[:, 8:1024], nwk, ALU.mult); rel_stg(s)
        s = stage(win_d[rows, 3072:3592])
        ts(wtok[:, k, 1528:1536], s[:, 0:8], nwk, ALU.mult)
        ts(wf[:, k, 1536:1792], s[:, 8:264], nwk, ALU.mult)
        ts(wtok[:, k, 1536:1792], s[:, 264:520], nwk, ALU.mult); rel_stg(s)
        s = stage(wout_d[rows, :]);        ts(wo[:, k, :], s[:, 0:1024], rs[:, k:k + 1], ALU.mult); rel_stg(s)
        s = stage(wmem_d[rows, :]);        ts(wm[:, k, :], s[:, 0:512], sm[:, 8 + k:9 + k], ALU.mult); rel_stg(s)

    tile_idx = [0]

    def norm_transpose(src, dstT, dst_cols, stc):
        act(hb.ap(), src, AF.Square, accum=stc("ssq"))
        ts(stc("tmp")[:, 0:1], stc("ssq"), 1.0 / D, ALU.mult, EPS, ALU.add)
        rsqrt(stc("rstd"), stc("tmp")[:, 0:1])
        act(hb.ap(), src, AF.Identity, scale=stc("rstd"))
        bk = bank()
        bv = bk.ap().bitcast(BF16)
        pe([("tr", bv[:, k * 128:(k + 1) * 128], hb[:, k * 128:(k + 1) * 128], identb.ap()) for k in range(8)],
           r=[hb, identb], w=[bk])
        cp("dve", dstT[:, :, dst_cols], bv.rearrange("p (k t) -> p k t", k=8))
        rel(bk)

    def seq_prologue(sq):
        mslot = free_stg.pop(0)
        mhT = mslot.ap().bitcast(BF16).rearrange("p (k m) -> p k m", k=8)
        for mc in range(2):
            s = stage(mem_d[sq * MEM + mc * 128: sq * MEM + (mc + 1) * 128, :])
            norm_transpose(s.ap(), mhT, slice(mc * 128, (mc + 1) * 128), stc)
            rel_stg(s)
        for pair in range(2):
            bk = bank()
            pe([("mm", bk[:, 0:256], wm[:, k, pair * 128:(pair + 1) * 128], mhT[:, k, :], k == 0, k == 7)
                for k in range(8)], r=[wm, mhT], w=[bk])
            ts(kmT[:, pair, :], bk[:, 0:256], 0.125, ALU.mult)
            rel(bk)
        memset("dve", vme.ap(), 1.0)
        for mc in range(2):
            bk = bank()
            pe([("mm", bk[:, 0:256], mhT[:, k, mc * 128:(mc + 1) * 128], wm[:, k, 256:512], k == 0, k == 7)
                for k in range(8)], r=[wm, mhT], w=[bk])
            cp("dve", vme[:, mc, :, 0:64], bk[:, 0:256].rearrange("p (h d) -> p h d", h=4))
            rel(bk)
        rel_stg(mslot)
        memset("dve", S32.ap(), 0.0)
        memset("dve", Sbf.ap(), 0.0)
        memset("dve", R32.ap(), 0.0)
        memset("dve", Rbf.ap(), 0.0)

    cur_xs = [None]
    BSTEP = 1

    def tile_body(sq, T):
        ti = tile_idx[0]
        tile_idx[0] += 1
        pp = ti % 2
        stc = lambda name: stb[pp][name].ap()
        v2, kd, ke, qkT, gz = v2_[pp], kd_[pp], ke_[pp], qkT_[pp], gz_[pp]
        qkb, vb, vbz, qcT = qkb_[pp], vb_[pp], vbz_[pp], qcT_[pp]
        row0 = sq * S + T * 128
        xs, rp = pending.pop(0)
        cur_xs[0] = xs
        if ti + 1 < len(order):
            prefetch(ti + 1)
        yield
        pc, pcprev = pcT[ti % 2], pcT[(ti + 1) % 2]
        norm_transpose(xs.ap(), hT, slice(0, 128), stc)
        yield
        bz, bqk, bvz, bxz = bank(), bank(), bank(), bank()
        for bk, c0, n in ((bz, 0, 512), (bqk, 512, 512), (bvz, 1024, 512), (bxz, 1536, 264)):
            pe([("mm", bk[:, 0:n], hT[:, k, :], wtok[:, k, c0:c0 + n], k == 0, k == 7) for k in range(8)],
               r=[hT, wtok], w=[bk])
            yield
        yield
        for bk, n, c0 in ((bz, 512, 0), (bvz, 256, 512), (bxz, 256, 768)):
            src = bk[:, 0:n] if bk is not bvz else bk[:, 256:512]
            act(th[:, 0:n], src, AF.Tanh, scale=0.5)
            stt(gz[:, c0:c0 + n], th[:, 0:n], 1.0, src, ALU.add, ALU.mult)
        yield
        tt("dve", stc("xa"), bxz[:, 260:264], sm2[:, 4:8], ALU.add)
        act(stc("tb"), bxz[:, 256:260], AF.Tanh, scale=0.5)
        act(stc("e"), stc("xa"), AF.Exp)
        act(stc("sp"), stc("e"), AF.Ln, bias=1.0)
        tt("dve", stc("g"), stc("sp"), negA.ap(), ALU.mult)
        ts(stc("hb"), stc("tb"), 0.25, ALU.mult, 0.25, ALU.add)
        ts(stc("nb"), stc("tb"), -0.5, ALU.mult, -0.5, ALU.add)
        yield
        cp("act", qkb.ap(), bqk.ap())
        cp("act", vb.ap(), bvz[:, 0:256])
        tt("dve", vbz.ap().rearrange("p (h d) -> p h d", h=4), vb.ap().rearrange("p (h d) -> p h d", h=4),
           C("zeta").unsqueeze(2).to_broadcast([128, 4, 64]), ALU.mult)
        rel(bz, bqk, bvz, bxz)
        yield
        bq = bank()
        for pair in range(2):
            pe([("mm", bq[:, pair * 128:(pair + 1) * 128], wf[:, k, 1536 + pair * 128:1536 + (pair + 1) * 128],
                 hT[:, k, :], k == 0, k == 7) for k in range(8)], r=[hT, wf], w=[bq])
        for h2 in range(2):
            ts(qcT[:, h2 * 2:h2 * 2 + 2, :], bq[:, 0:256].rearrange("p (a t) -> p a t", a=2), C("hm")[:, h2:h2 + 1], ALU.mult)
        rel(bq)
        yield
        if T == 0:
            memset("dve", pc[:, :, 0:3], 0.0)
        else:
            cp("dve", pc[:, :, 0:3], pcprev[:, :, 128:131])
        bcv = []
        for grp in range(3):
            bk = bank()
            for j in range(4):
                cch = grp * 4 + j
                pe([("mm", bk[:, j * 128:(j + 1) * 128], wf[:, k, cch * 128:(cch + 1) * 128], hT[:, k, :], k == 0, k == 7)
                    for k in range(8)], r=[hT, wf], w=[bk])
                if j % 2 == 1:
                    yield
            cp("act" if grp != 1 else "dve", pc[:, grp * 4:(grp + 1) * 4, 3:131], bk.ap().rearrange("p (j t) -> p j t", j=4))
            rel(bk)
        for grp in range(3):
            bk = bank()
            for j in range(4):
                cch = grp * 4 + j
                pe([("mm", bk[:, j * 128:(j + 1) * 128], pc[:, cch, t:t + 128], cdiag[:, cch * 4 + t, :], t == 0, t == 3)
                    for t in range(4)], r=[pc, cdiag], w=[bk])
            bcv.append(bk)
            yield
        for grp in range(3):
            bk = bcv[grp]
            act(th.ap(), bk.ap(), AF.Tanh, scale=0.5)
            dst = q2k2[:, grp * 512:(grp + 1) * 512] if grp < 2 else v2.ap()
            stt(dst, th.ap(), 1.0, bk.ap(), ALU.add, ALU.mult)
            rel(bk)
            yield
        yield
        for j in range(8):
            act(junkF[:, 0:128], q2k2[:, j * 128:(j + 1) * 128], AF.Square, accum=stc("ss8")[:, j:j + 1])
        ts(stc("ss8"), stc("ss8"), 4.0 * EPS, ALU.add)
        rsqrt(stc("r8"), stc("ss8"))
        ts(stc("r8")[:, 0:4], stc("r8")[:, 0:4], 128.0 ** -0.5, ALU.mult)
        yield
        bg = bank()
        pe([("mm", bg[:, 0:4], C("ule"), stc("g"), True, True),
            ("mm", bg[:, 4:8], C("ones"), stc("g"), True, True)], r=[cst, stc("g")], w=[bg])
        cp("dve", stc("G"), bg[:, 0:4])
        act(stc("eG"), bg[:, 0:4], AF.Exp)
        act(stc("eGl"), bg[:, 4:8], AF.Exp)
        tt("dve", stc("dG"), bg[:, 4:8], stc("G"), ALU.subtract)
        act(stc("eGd"), stc("dG"), AF.Exp)
        rel(bg)
        tt("dve", stc("sq"), stc("r8")[:, 0:4], stc("eG"), ALU.mult)
        tt("dve", stc("skd"), stc("r8")[:, 4:8], stc("eGd"), ALU.mult)
        tt("dve", stc("ske"), stc("r8")[:, 4:8], stc("eG"), ALU.mult)

        def bc4(a):
            return a.unsqueeze(2).to_broadcast([128, 4, 128])

        def v3(a):
            return a.rearrange("p (h d) -> p h d", h=4)
        q2, k2 = q2k2[:, 0:512], q2k2[:, 512:1024]
        tt("pool", v3(qh.ap()), v3(q2), bc4(stc("r8")[:, 0:4]), ALU.mult)
        tt("dve", v3(kh.ap()), v3(k2), bc4(stc("r8")[:, 4:8]), ALU.mult)
        tt("pool", v3(qt.ap()), v3(q2), bc4(stc("sq")), ALU.mult)
        tt("pool", v3(kd.ap()), v3(k2), bc4(stc("skd")), ALU.mult)
        tt("pool", v3(ke.ap()), v3(k2), bc4(stc("ske")), ALU.mult)
        yield
        for i, srcb in enumerate((qh, kh, qt)):
            bk = bank()
            bv = bk.ap().bitcast(BF16)
            pe([("tr", bv[:, h * 128:(h + 1) * 128], srcb[:, h * 128:(h + 1) * 128], identb.ap()) for h in range(4)],
               r=[srcb, identb], w=[bk])
            cp("act" if i != 1 else "dve", qkT[:, i * 4:(i + 1) * 4, :], bv[:, 0:512].rearrange("p (h t) -> p h t", h=4))
            rel(bk)
            yield
        yield 'MID'
        tt("pool", Rg.ap(), C("ule").unsqueeze(1).to_broadcast([128, 4, 128]), bc4(stc("g")), ALU.mult)
        bd = bank()
        pe([("mm", bd.ap(), C("mgt"), Rg.ap().rearrange("p h c -> p (h c)"), True, True)], r=[cst, Rg], w=[bd])
        act(gts.ap(), bd.ap(), AF.Exp)
        rel(bd)
        tt("dve", v3(gti.ap()), v3(gts.ap()), C("mui").unsqueeze(1).to_broadcast([128, 4, 128]), ALU.mult)
        tt("pool", v3(gts.ap()), v3(gti.ap()), C("mus").unsqueeze(1).to_broadcast([128, 4, 128]), ALU.mult)
        yield
        bkk, bqkT = bank(), bank()
        pe([("mm", bkk[:, h * 128:(h + 1) * 128], qkT[:, 4 + h, :], qkT[:, 4 + h, :], True, True) for h in range(4)],
           r=[qkT], w=[bkk])
        pe([("mm", bqkT[:, h * 128:(h + 1) * 128], qkT[:, 4 + h, :], qkT[:, h, :], True, True) for h in range(4)],
           r=[qkT], w=[bqkT])
        identf = C("ident")
        for hp in range(2):
            A0 = APp[0][hp]
            for hh in range(2):
                h = hp * 2 + hh
                stt(A0[:, hh, 0, :], bkk[:, h * 128:(h + 1) * 128], stc("nb")[:, h:h + 1], gts[:, h * 128:(h + 1) * 128],
                    ALU.mult, ALU.mult)
            tt("dve", A0[:, :, 1, :], A0[:, :, 0, :], identf.unsqueeze(1).to_broadcast([128, 2, 128]), ALU.add)
        tt("dve", PT.ap(), v3(bqkT.ap()), v3(gti.ap()), ALU.mult)
        rel(bkk, bqkT)
        yield
        for hp in range(2):
            A0 = APp[0][hp]
            bb = bank()
            pe([("tr", bb[:, hh * 128:(hh + 1) * 128], A0[:, hh, 0, :], identf) for hh in range(2)], r=[A0, cst], w=[bb])
            cp("act", Bmp[0][hp].ap(), bb[:, 0:256].rearrange("p (h c) -> p h c", h=2))
            rel(bb)
        cur = 0
        for lvl in range(0, 7):
            last = lvl == 6
            for hp in range(2):
                Ak, Bk = APp[cur][hp], Bmp[cur][hp]
                An, Bn = APp[1 - cur][hp], Bmp[1 - cur][hp]
                if not last:
                    b3 = bank()
                    pe([("mm", b3[:, hh * 128:(hh + 1) * 128], Ak[:, hh, 0, :], Bk[:, hh, :], True, True) for hh in range(2)],
                       r=[Bk, Ak], w=[b3])
                    cp("act", Bn.ap(), b3[:, 0:256].rearrange("p (h c) -> p h c", h=2))
                    rel(b3)
                b1 = bank()
                if lvl == 0:
                    pe([("mm", b1[:, hh * 256:hh * 256 + 128], Bk[:, hh, :], Ak[:, hh, 0, :], True, True) for hh in range(2)],
                       r=[Bk, Ak], w=[b1])
                elif last:
                    pe([("mm", b1[:, hh * 256 + 128:hh * 256 + 256], Bk[:, hh, :], Ak[:, hh, 1, :], True, True) for hh in range(2)],
                       r=[Bk, Ak], w=[b1])
                else:
                    pe([("mm", b1[:, hh * 256:(hh + 1) * 256], Bk[:, hh, :], Ak[:, hh, :, :].rearrange("p a c -> p (a c)"),
                         True, True) for hh in range(2)], r=[Bk, Ak], w=[b1])
                bv = b1.ap().rearrange("p (h a c) -> p h a c", h=2, a=2)
                if lvl == 0:
                    cp("act", An[:, :, 0, :], bv[:, :, 0, :])
                    cp("dve", An[:, :, 1, :], Ak[:, :, 1, :])
                elif last:
                    tt("dve", TT[:, hp * 2:hp * 2 + 2, :], Ak[:, :, 1, :], bv[:, :, 1, :], ALU.add)
                else:
                    cp("act", An[:, :, 0, :], bv[:, :, 0, :])
                    tt("dve", An[:, :, 1, :], Ak[:, :, 1, :], bv[:, :, 1, :], ALU.add)
                rel(b1)
            cur = 1 - cur
        yield
        bu_ps, bw_ps = bank(), bank()
        pe([("mm", bu_ps[:, h * 128:(h + 1) * 128], TT[:, h, :], v2[:, h * 128:(h + 1) * 128], True, True) for h in range(4)],
           r=[TT, v2], w=[bu_ps])
        pe([("mm", bw_ps[:, h * 128:(h + 1) * 128], ke[:, h * 128:(h + 1) * 128], TT[:, h, :], True, True) for h in range(4)],
           r=[TT, ke], w=[bw_ps])
        yield
        tt("dve", v3(bu.ap()), v3(bu_ps.ap()), bc4(stc("hb")), ALU.mult)
        cp("act", wT.ap(), v3(bw_ps.ap()))
        rel(bu_ps, bw_ps)
        yield
        bws, bo, bds = bank(), bank(), bank()
        pe([("mm", bws[:, h * 128:(h + 1) * 128], wT[:, h, :], Sbf[:, h, :], True, True) for h in range(4)],
           r=[wT, Sbf], w=[bws])
        for h in range(4):
            hs = slice(h * 128, (h + 1) * 128)
            stt(vn[:, hs], bws[:, hs], stc("nb")[:, h:h + 1], bu[:, hs], ALU.mult, ALU.add)
        rel(bws)
        yield
        for h in range(4):
            hs = slice(h * 128, (h + 1) * 128)
            pe([("mm", bo[:, hs], qkT[:, 8 + h, :], Sbf[:, h, :], True, False),
                ("mm", bo[:, hs], PT[:, h, :], vn[:, hs], False, True)], r=[qkT, Sbf, PT, vn], w=[bo])
        pe([("mm", bds[:, h * 128:(h + 1) * 128], kd[:, h * 128:(h + 1) * 128], vn[:, h * 128:(h + 1) * 128], True, True)
            for h in range(4)], r=[kd, vn], w=[bds])
        for h in range(4):
            hs = slice(h * 128, (h + 1) * 128)
            stt(S32[:, h, :], S32[:, h, :], stc("eGl")[:, h:h + 1], bds[:, hs], ALU.mult, ALU.add)
        cp("act", Sbf.ap(), S32.ap())
        rel(bds)
        yield
        for h in range(4):
            act(junkB[:, 0:128], bo[:, h * 128:(h + 1) * 128], AF.Square, accum=stc("so")[:, h:h + 1])
        ts(stc("so"), stc("so"), 1.0 / 128.0, ALU.mult, EPS, ALU.add)
        rsqrt(stc("ro"), stc("so"))
        for h in range(4):
            hs = slice(h * 128, (h + 1) * 128)
            stt(y[:, hs], bo[:, hs], stc("ro")[:, h:h + 1], gz[:, hs], ALU.mult, ALU.mult)
        rel(bo)

        yield
        rv = lambda a: a.rearrange("p (g i two) -> p g i two", g=8, two=2)
        cosb = rp[:, 0:64].unsqueeze(1).to_broadcast([128, 8, 64])
        tt("pool", r1.ap().rearrange("p (g d) -> p g d", g=8), qkb.ap().rearrange("p (g d) -> p g d", g=8), cosb, ALU.mult)
        nse = rp[:, 64:96].unsqueeze(1).to_broadcast([128, 8, 32])
        so_ = rp[:, 96:128].unsqueeze(1).to_broadcast([128, 8, 32])
        tt("dve", rv(r2.ap())[:, :, :, 0], rv(qkb.ap())[:, :, :, 1], nse, ALU.mult)
        tt("dve", rv(r2.ap())[:, :, :, 1], rv(qkb.ap())[:, :, :, 0], so_, ALU.mult)
        tt("dve", qkr.ap(), r1.ap(), r2.ap(), ALU.add)
        yield
        bk = bank()
        bv = bk.ap().bitcast(BF16)
        pe([("tr", bv[:, j * 128:(j + 1) * 128], qkr[:, j * 128:(j + 1) * 128], identb.ap()) for j in range(4)],
           r=[qkr, identb], w=[bk])
        cp("act", rT[:, 0:2, :], bv[:, 256:512].rearrange("p (j t) -> p j t", j=2))
        for h2 in range(2):
            ts(rT[:, 2 + h2 * 2:4 + h2 * 2, :], bv[:, 0:256].rearrange("p (j t) -> p j t", j=2), C("hm")[:, h2:h2 + 1], ALU.mult)
            tt("dve", rT[:, 6 + h2 * 2:8 + h2 * 2, :], bv[:, 0:256].rearrange("p (j t) -> p j t", j=2),
               C("xi%d" % h2).rearrange("p (j t) -> p j t", j=2), ALU.mult)
        rel(bk)
        yield
        bsc = bank()
        items = []
        for h in range(4):
            pr, h2 = h // 2, h % 2
            prt = slice(h2 * 64, (h2 + 1) * 64)
            items.append(("mm", bsc[:, h * 128:(h + 1) * 128], rT[:, pr, :], rT[:, 2 + h2 * 2 + pr, :], True, True))
        pe(items, r=[rT], w=[bsc])
        tt("dve", PbT.ap(), v3(bsc.ap()), v3(C("dmat")), ALU.mult)
        rel(bsc)
        yield
        bob = bank()
        for h in range(4):
            pr, h2 = h // 2, h % 2
            prt = slice(h2 * 64, (h2 + 1) * 64)
            pe([("mm", bob[:, h * 64:(h + 1) * 64], PbT[:, h, :], vb[:, h * 64:(h + 1) * 64], True, False),
                ("mm", bob[:, h * 64:(h + 1) * 64], rT[:, 6 + h2 * 2 + pr, :], Rbf[:, pr, :], False, True)],
               r=[PbT, vb, rT, Rbf], w=[bob])
        yield
        bkv = bank()
        pe([("mm", bkv[:, pr * 128:(pr + 1) * 128], qkr[:, 256 + pr * 128:256 + (pr + 1) * 128],
             vbz[:, pr * 128:(pr + 1) * 128], True, True) for pr in range(2)], r=[qkr, vbz], w=[bkv])
        yield
        for pr in range(2):
            for h2 in range(2):
                prt = slice(h2 * 64, (h2 + 1) * 64)
                stt(R32[prt, pr, :], R32[prt, pr, :], C("dec")[prt, pr:pr + 1],
                    bkv[prt, pr * 128 + h2 * 64: pr * 128 + (h2 + 1) * 64], ALU.mult, ALU.add)
        cp("act", Rbf.ap(), R32.ap())
        rel(bkv)
        yield
        ob3 = bob[:, 0:256].rearrange("p (h d) -> p h d", h=4)
        tmp2 = stc("tmp2")
        P.op("dve", lambda e: e.tensor_reduce(out=_ap(tmp2), in_=_ap(ob3), axis=mybir.AxisListType.X, op=ALU.add),
             r=[bob], w=[tmp2])
        ts(tmp2, tmp2, -1.0 / 64.0, ALU.mult)
        tt("dve", ynb.ap().rearrange("p (h d) -> p h d", h=4), ob3,
           tmp2.unsqueeze(2).to_broadcast([128, 4, 64]), ALU.add)
        rel(bob)
        for h in range(4):
            act(junkB[:, 0:64], ynb[:, h * 64:(h + 1) * 64], AF.Square, accum=stc("so2")[:, h:h + 1])
        ts(stc("so2"), stc("so2"), 1.0 / 64.0, ALU.mult, EPS, ALU.add)
        rsqrt(stc("ro2"), stc("so2"))
        for h in range(4):
            hs = slice(h * 64, (h + 1) * 64)
            stt(y[:, 512 + h * 64:512 + (h + 1) * 64], ynb[:, hs], stc("ro2")[:, h:h + 1],
                gz[:, 512 + h * 64:512 + (h + 1) * 64], ALU.mult, ALU.mult)

        yield
        bl1, bl2 = bank(), bank()
        for hp in range(2):
            bk = (bl1, bl2)[hp]
            items = []
            for hh in range(2):
                h = hp * 2 + hh
                pr, h2 = h // 2, h % 2
                prt = slice(h2 * 64, (h2 + 1) * 64)
                for mc in range(2):
                    items.append(("mm", bk[:, (hh * 2 + mc) * 128:(hh * 2 + mc + 1) * 128],
                                  kmT[:, pr, mc * 128:(mc + 1) * 128], qcT[:, h2 * 2 + pr, :], True, True))
            pe(items, r=[kmT, qcT], w=[bk])
            act(pT[:, hp * 4:(hp + 1) * 4, :], bk.ap().rearrange("p (j t) -> p j t", j=4), AF.Exp)
            rel(bk)
            yield
        boc = bank()
        for h in range(4):
            pe([("mm", boc[:, h * 65:(h + 1) * 65], pT[:, h * 2 + mc, :], vme[:, mc, h, :], mc == 0, mc == 1)
                for mc in range(2)], r=[pT, vme], w=[boc])
        oc3 = boc[:, 0:260].rearrange("p (h d) -> p h d", h=4)
        rden = stc("rden")
        P.op("dve", lambda e: e.reciprocal(out=_ap(rden), in_=_ap(oc3[:, :, 64])), r=[boc], w=[rden])
        for h in range(4):
            stt(y[:, 768 + h * 64:768 + (h + 1) * 64], boc[:, h * 65:h * 65 + 64], stc("rden")[:, h:h + 1],
                gz[:, 768 + h * 64:768 + (h + 1) * 64], ALU.mult, ALU.mult)
        rel(boc)

        yield
        bk = bank()
        bv = bk.ap().bitcast(BF16)
        pe([("tr", bv[:, k * 128:(k + 1) * 128], y[:, k * 128:(k + 1) * 128], identb.ap()) for k in range(8)],
           r=[y, identb], w=[bk])
        cp("act", yT.ap(), bv.rearrange("p (k t) -> p k t", k=8))
        rel(bk)
        yield
        for half in range(2):
            bk = bank()
            pe([("mm", bk.ap(), yT[:, k, :], wo[:, k, half * 512:(half + 1) * 512], k == 0, k == 7) for k in range(8)],
               r=[yT, wo], w=[bk])
            tt("dve", xs[:, half * 512:(half + 1) * 512], bk.ap(), xs[:, half * 512:(half + 1) * 512], ALU.add)
            rel(bk)
        act(y.ap(), xs.ap(), AF.Square, accum=stc("ssq2"))
        ts(stc("tmp2")[:, 0:1], stc("ssq2"), 1.0 / D, ALU.mult, EPS, ALU.add)
        rsqrt(stc("rstd2"), stc("tmp2")[:, 0:1])
        stt(xs.ap(), xs.ap(), stc("rstd2"), fw.ap(), ALU.mult, ALU.mult)
        last_store[xs.name] = P.dma(out_d[row0:row0 + 128, :], xs.ap())
        rel_stg(xs)

    order = [(sq, T) for sq in range(NSEQ) for T in range(NT)]
    pending = []
    last_store = {}

    def prefetch(i):
        sq_, T_ = order[i]
        r0 = sq_ * S + T_ * 128
        xs_ = stage(x_d[r0:r0 + 128, :])
        rp_ = rope[i % 4]
        P.dma(rp_.ap(), rope_d[T_])
        pending.append((xs_, rp_))

    def adv(g):
        try:
            return next(g)
        except StopIteration:
            return "END"

    PIPE = True
    for sq in range(NSEQ):
        seq_prologue(sq)
        if sq == 0:
            prefetch(0)
        prev = None
        for T in range(NT):
            g = tile_body(sq, T)
            fdone, bdone = False, prev is None
            while not (fdone and bdone):
                if not bdone:
                    for _ in range(BSTEP):
                        if adv(prev) == "END":
                            bdone = True
                            break
                if not fdone:
                    if adv(g) == "MID":
                        fdone = True
            if PIPE:
                prev = g
            else:
                while adv(g) != "END":
                    pass
                prev = None
        if prev is not None:
            while adv(prev) != "END":
                pass
    P.finish_wait("sp", list(last_store.values()))
    import os as _os
    if _os.environ.get("KSTATS"):
        print("op counts", P.cnt, {e: len(v) for e, v in P.stream.items()},
              {e: sum(len(w[0]) for w in v) for e, v in P.stream.items()}, flush=True)
    with nc.allow_non_contiguous_dma(reason="tiny param loads"):
        P.emit()
    es.close()
    return nc, cat, rope_np


_CACHE = {}


def _get(NSEQ, S):
    key = (NSEQ, S)
    if key not in _CACHE:
        _CACHE[key] = build(NSEQ, S)
    return _CACHE[key]


def run(inputs, NSEQ, S, ncores):
    nc, cat, rope_np = _get(NSEQ, S)
    f = lambda a: np.ascontiguousarray(np.asarray(a, dtype=np.float32))
    x = f(inputs["x"]); mem = f(inputs["mem"])
    shared = {
        "w_in": f(inputs["w_in"][0]), "w_out": f(inputs["w_out"][0]), "w_mem_kv": f(inputs["w_mem_kv"][0]),
        "norm_w": f(inputs["norm_w"][0]), "mem_norm_w": f(inputs["mem_norm_w"][0]),
        "gdn_conv_w": f(inputs["gdn_conv_w"][0]), "gdn_A_log": f(inputs["gdn_A_log"][0]),
        "gdn_dt_bias": f(inputs["gdn_dt_bias"][0]), "gdn_norm_w": f(inputs["gdn_norm_w"][0]),
        "ret_gn_w": f(inputs["ret_gn_w"][0]), "final_norm_w": f(inputs["final_norm_w"]),
        "cst": cat, "rope": rope_np,
    }
    in_maps = []
    for c in range(ncores):
        m = dict(shared)
        m["x"] = np.ascontiguousarray(x[c * NSEQ:(c + 1) * NSEQ].reshape(NSEQ * S, D))
        m["mem"] = np.ascontiguousarray(mem[c * NSEQ:(c + 1) * NSEQ].reshape(NSEQ * MEM, D))
        in_maps.append(m)
    res = run_bass_kernel_spmd(nc, in_maps, core_ids=list(range(ncores)))
    outs = [np.asarray(r["out"]).reshape(NSEQ, S, D) for r in res.results]
    return np.concatenate(outs, axis=0).astype(np.float32)


def kernel(**inputs):
    B, S, _ = inputs["x"].shape
    NSEQ = B // NCORES
    return run(inputs, NSEQ, S, NCORES)
```
